# Optimizing a Trainium2 kernel written in Bass

```python
import jax, jax.numpy as jnp
from jax import lax
import numpy as np


D_MODEL = 1024
BATCH = 4
SEQ = 4096
DEPTH = 2
DEC_BATCH = 32
DEC_SEQ = 1
PAST_LEN = 8192
PAGE_SIZE = 128

N_EVEN = (DEPTH + 1) // 2
N_ODD = DEPTH // 2
EPS = 1e-6
W_A = D_MODEL
CONV_W = 3
H_B = 16
HD_B = 64
W_B = H_B * HD_B
DIL_PATTERNS = ((128, 1), (512, 4), (2048, 16))
MAX_WINDOW = 2048
POOL_SIZES = (2, 4, 8, 16)
N_POOL = 4
W_C = D_MODEL
G_C = W_C // N_POOL
POOL_MAX = 16
H_D = 4
DK_D = 256
DV_D = 256
W_D = H_D * DV_D
RET_CHUNK = 128

kernel_name = 'hybrid_conv_dilattn_pool_retention_step'


def rmsnorm(x, g):
    xf = x.astype(jnp.float32)
    y = xf * lax.rsqrt(jnp.mean(xf * xf, axis=-1, keepdims=True) + EPS)
    return (y * g.astype(jnp.float32)).astype(x.dtype)


def adaln(c, w, b):
    mod = (jax.nn.silu(c) @ w + b)[:, None, :]
    return jnp.split(mod, 3, axis=-1)


def split_cols(z, widths):
    idx = np.cumsum(widths)[:-1].tolist()
    return jnp.split(z, idx, axis=-1)


def alibi_slopes(n):
    return jnp.asarray(2.0 ** (-8.0 * np.arange(1, n + 1) / n), dtype=jnp.float32)


def retention_log_decay():
    return jnp.asarray(np.log(1.0 - 2.0 ** (-5.0 - np.arange(H_D))), dtype=jnp.float32)


def short_conv(bg, cg, xv, cw, cb, prev):
    u = cg * xv
    ext = jnp.concatenate([prev.astype(u.dtype), u], axis=1)
    L = u.shape[1]
    conv = cb + sum(ext[:, j:j + L] * cw[j] for j in range(CONV_W))
    return bg * conv, ext[:, L:]


def combine_patterns(outs, lses):
    wts = jax.nn.softmax(jnp.stack(lses), axis=0)
    return jnp.sum(wts[..., None] * jnp.stack(outs), axis=0)


def dilated_attn_prompt(q, k, v, slopes):
    Bn, S, H, Dh = q.shape
    qf = q.astype(jnp.float32) * (Dh ** -0.5)
    kf = k.astype(jnp.float32)
    vf = v.astype(jnp.float32)
    outs, lses = [], []
    for w, d in DIL_PATTERNS:
        nk = w // d
        span = nk * d
        sp = -(-S // span) * span
        nb = sp // span
        pad = ((0, 0), (0, sp - S), (0, 0), (0, 0))
        qs, ks, vs = (jnp.pad(a, pad).reshape(Bn, nb, nk, d, H, Dh) for a in (qf, kf, vf))

        def with_prev(a):
            prev = jnp.pad(a, ((0, 0), (1, 0), (0, 0), (0, 0), (0, 0), (0, 0)))[:, :nb]
            return jnp.concatenate([prev, a], axis=2)

        kk, vv = with_prev(ks), with_prev(vs)
        s = jnp.einsum('bnirhd,bnjrhd->bnrhij', qs, kk)
        qi = jnp.arange(nk)[:, None]
        kj = jnp.arange(2 * nk)[None, :]
        dist = nk + qi - kj
        blk = jnp.arange(nb)[:, None, None]
        valid = (dist >= 0) & (dist <= nk) & (blk * nk + kj - nk >= 0)
        bias = -slopes[:, None, None] * (dist * d).astype(jnp.float32)
        s = jnp.where(valid[None, :, None, None], s + bias, -jnp.inf)
        m = jnp.max(s, axis=-1, keepdims=True)
        p = jnp.exp(s - m)
        l = jnp.sum(p, axis=-1)
        o = jnp.einsum('bnrhij,bnjrhd->bnirhd', p, vv) / jnp.transpose(l, (0, 1, 4, 2, 3))[..., None]
        lse = jnp.transpose(m[..., 0] + jnp.log(l), (0, 1, 4, 2, 3))
        outs.append(o.reshape(Bn, sp, H, Dh)[:, :S])
        lses.append(lse.reshape(Bn, sp, H)[:, :S])
    return combine_patterns(outs, lses)


def dilated_attn_decode(q, k, v, buf_k, buf_v, slopes):
    Bn, L, H, Dh = q.shape
    wb = buf_k.shape[1]
    qf = q.astype(jnp.float32) * (Dh ** -0.5)
    allk = jnp.concatenate([buf_k.astype(jnp.float32), k.astype(jnp.float32)], axis=1)
    allv = jnp.concatenate([buf_v.astype(jnp.float32), v.astype(jnp.float32)], axis=1)
    outs, lses = [], []
    for w, d in DIL_PATTERNS:
        nk = w // d
        steps = jnp.arange(nk + 1)
        idx = wb + jnp.arange(L)[:, None] - steps[None, :] * d
        valid = idx >= 0
        idx = jnp.maximum(idx, 0)
        kg, vg = allk[:, idx], allv[:, idx]
        s = jnp.einsum('blhd,blkhd->blhk', qf, kg) - slopes[:, None] * (steps * d).astype(jnp.float32)
        s = jnp.where(valid[None, :, None, :], s, -jnp.inf)
        m = jnp.max(s, axis=-1, keepdims=True)
        p = jnp.exp(s - m)
        l = jnp.sum(p, axis=-1)
        outs.append(jnp.einsum('blhk,blkhd->blhd', p, vg) / l[..., None])
        lses.append(m[..., 0] + jnp.log(l))
    return combine_patterns(outs, lses)


def pool_mix(u, prev, pos0, pw, ps):
    Bn, L, _ = u.shape
    P = POOL_MAX - 1
    ext = jnp.concatenate([prev.astype(u.dtype), u], axis=1)
    ef = ext.astype(jnp.float32)
    cs = jnp.concatenate([jnp.zeros((Bn, 1, W_C), jnp.float32), jnp.cumsum(ef, axis=1)], axis=1)
    pos = (pos0 + jnp.arange(L)).astype(jnp.float32)[None, :, None]
    hi = cs[:, P + 1:P + 1 + L]
    groups = []
    for gi, w in enumerate(POOL_SIZES):
        sl = slice(gi * G_C, (gi + 1) * G_C)
        lo = cs[:, P + 1 - w:P + 1 - w + L, sl]
        cnt = jnp.minimum(float(w), pos + 1.0)
        groups.append((hi[..., sl] - lo) / cnt - ef[:, P:, sl])
    pooled = jnp.stack(groups, axis=2)
    mixed = jnp.einsum('blgc,gce->blge', pooled, pw.astype(jnp.float32)).reshape(Bn, L, W_C)
    return (mixed * ps.astype(jnp.float32)).astype(u.dtype), ext[:, L:]


def retention_chunk(state, q, k, v, log_g):
    L = q.shape[1]
    t = jnp.arange(L, dtype=jnp.float32)
    diff = t[:, None] - t[None, :]
    decay = jnp.where(diff >= 0, jnp.exp(jnp.maximum(diff, 0.0)[None] * log_g[:, None, None]), 0.0)
    scores = jnp.einsum('bihk,bjhk->bhij', q, k) * decay
    inner = jnp.einsum('bhij,bjhv->bihv', scores, v)
    cross = jnp.einsum('bihk,bhkv->bihv', q, state) * jnp.exp((t + 1.0)[:, None] * log_g[None, :])[None, :, :, None]
    kd = k * jnp.exp((L - 1.0 - t)[:, None] * log_g[None, :])[None, :, :, None]
    new_state = jnp.exp(L * log_g)[None, :, None, None] * state + jnp.einsum('bjhk,bjhv->bhkv', kd, v)
    return new_state, inner + cross


def retention_prompt(q, k, v, log_g):
    Bn, S, H, Dk = q.shape
    nc = S // RET_CHUNK

    def chunks(a):
        return a.reshape(Bn, nc, RET_CHUNK, H, a.shape[-1]).transpose(1, 0, 2, 3, 4)

    s0 = jnp.zeros((Bn, H, Dk, v.shape[-1]), jnp.float32)
    s_fin, o = lax.scan(lambda st, xs: retention_chunk(st, xs[0], xs[1], xs[2], log_g), s0,
                        (chunks(q), chunks(k), chunks(v)))
    return s_fin, o.transpose(1, 0, 2, 3, 4).reshape(Bn, S, H, v.shape[-1])


def even_layer(x, c, g, aw, ab, w_in, cw, cb, w_out, conv_prev, buf_k, buf_v):
    Bn, L, _ = x.shape
    shift, scale, gate = adaln(c, aw, ab)
    h = rmsnorm(x, g) * (1 + scale) + shift
    z = h @ w_in
    bg, cg, xv, ga, q, k, v, gb = split_cols(z, [W_A, W_A, W_A, W_A, W_B, W_B, W_B, W_B])
    if conv_prev is None:
        conv_prev = jnp.zeros((Bn, CONV_W - 1, W_A), x.dtype)
    y_a, conv_new = short_conv(bg, cg, xv, cw, cb, conv_prev)
    y_a = y_a * jax.nn.silu(ga)
    q, k, v = (a.reshape(Bn, L, H_B, HD_B) for a in (q, k, v))
    slopes = alibi_slopes(H_B)
    if buf_k is None:
        o = dilated_attn_prompt(q, k, v, slopes)
        keep = min(MAX_WINDOW, L)
        k_new, v_new = k[:, L - keep:], v[:, L - keep:]
    else:
        o = dilated_attn_decode(q, k, v, buf_k, buf_v, slopes)
        k_new, v_new = k, v
    y_b = o.reshape(Bn, L, W_B).astype(x.dtype) * jax.nn.silu(gb)
    out = jnp.concatenate([y_a, y_b], axis=-1) @ w_out
    return x + gate * out, conv_new, k_new, v_new


def odd_layer(x, c, g, aw, ab, w_in, pw, ps, w_out, pool_prev, ret_prev, pos0):
    Bn, L, _ = x.shape
    shift, scale, gate = adaln(c, aw, ab)
    h = rmsnorm(x, g) * (1 + scale) + shift
    z = h @ w_in
    u, gc, q, k, v, gd = split_cols(z, [W_C, W_C, H_D * DK_D, H_D * DK_D, W_D, W_D])
    if pool_prev is None:
        pool_prev = jnp.zeros((Bn, POOL_MAX - 1, W_C), x.dtype)
    y_c, pool_new = pool_mix(u, pool_prev, pos0, pw, ps)
    y_c = y_c * jax.nn.silu(gc)
    qf = q.reshape(Bn, L, H_D, DK_D).astype(jnp.float32)
    kf = k.reshape(Bn, L, H_D, DK_D).astype(jnp.float32) * (DK_D ** -0.5)
    vf = v.reshape(Bn, L, H_D, DV_D).astype(jnp.float32)
    log_g = retention_log_decay()
    if ret_prev is None:
        ret_new, o = retention_prompt(qf, kf, vf, log_g)
    else:
        ret_new, o = retention_chunk(ret_prev.astype(jnp.float32), qf, kf, vf, log_g)
    o = o * lax.rsqrt(jnp.mean(o * o, axis=-1, keepdims=True) + EPS)
    y_d = o.reshape(Bn, L, W_D).astype(x.dtype) * jax.nn.silu(gd)
    out = jnp.concatenate([y_c, y_d], axis=-1) @ w_out
    return x + gate * out, pool_new, ret_new.astype(x.dtype)


def setup_inputs(seed: int = 0) -> dict:
    key = jax.random.key(seed)
    keys = jax.random.split(key, 32)
    cnt = [0]

    def nrm(shape, s):
        kk = keys[cnt[0]]
        cnt[0] += 1
        return jax.random.normal(kk, shape, jnp.float32) * s

    d = D_MODEL
    wb = min(MAX_WINDOW, PAST_LEN)
    in_e = 4 * W_A + 4 * W_B
    in_o = 2 * W_C + 2 * H_D * DK_D + 2 * W_D
    return {
        'x_prompt': nrm((BATCH, SEQ, d), 1.0),
        'x_sample': nrm((DEC_BATCH, DEC_SEQ, d), 1.0),
        'c_prompt': nrm((BATCH, d), 1.0),
        'c_sample': nrm((DEC_BATCH, d), 1.0),
        'state_conv': nrm((N_EVEN, DEC_BATCH, CONV_W - 1, W_A), 1.0),
        'cache_win_k': nrm((N_EVEN, DEC_BATCH, wb, H_B, HD_B), 1.0),
        'cache_win_v': nrm((N_EVEN, DEC_BATCH, wb, H_B, HD_B), 1.0),
        'state_pool': nrm((N_ODD, DEC_BATCH, POOL_MAX - 1, W_C), 1.0),
        'state_ret': nrm((N_ODD, DEC_BATCH, H_D, DK_D, DV_D), 0.1),
        'norm_e': 1.0 + nrm((N_EVEN, d), 0.02),
        'ada_w_e': nrm((N_EVEN, d, 3 * d), 0.3 * d ** -0.5),
        'ada_b_e': nrm((N_EVEN, 3 * d), 0.02),
        'w_in_e': nrm((N_EVEN, d, in_e), d ** -0.5),
        'conv_w': nrm((N_EVEN, CONV_W, W_A), CONV_W ** -0.5),
        'conv_b': nrm((N_EVEN, W_A), 0.02),
        'w_out_e': nrm((N_EVEN, W_A + W_B, d), (W_A + W_B) ** -0.5),
        'norm_o': 1.0 + nrm((N_ODD, d), 0.02),
        'ada_w_o': nrm((N_ODD, d, 3 * d), 0.3 * d ** -0.5),
        'ada_b_o': nrm((N_ODD, 3 * d), 0.02),
        'w_in_o': nrm((N_ODD, d, in_o), d ** -0.5),
        'pool_w': nrm((N_ODD, N_POOL, G_C, G_C), G_C ** -0.5),
        'pool_scale': 1.0 + nrm((N_ODD, W_C), 0.1),
        'w_out_o': nrm((N_ODD, W_C + W_D, d), (W_C + W_D) ** -0.5),
        'norm_f': 1.0 + nrm((d,), 0.02),
    }


def reference(x_prompt, x_sample, c_prompt, c_sample, state_conv, cache_win_k, cache_win_v,
              state_pool, state_ret, norm_e, ada_w_e, ada_b_e, w_in_e, conv_w, conv_b, w_out_e,
              norm_o, ada_w_o, ada_b_o, w_in_o, pool_w, pool_scale, w_out_o, norm_f):
    xp, xs = x_prompt, x_sample
    conv_p, conv_s, kp_l, ks_l, vp_l, vs_l = [], [], [], [], [], []
    pool_p, pool_s, ret_p, ret_s = [], [], [], []
    for layer in range(DEPTH):
        i = layer // 2
        if layer % 2 == 0:
            pe = (norm_e[i], ada_w_e[i], ada_b_e[i], w_in_e[i], conv_w[i], conv_b[i], w_out_e[i])
            xp, cst, kn, vn = even_layer(xp, c_prompt, *pe, None, None, None)
            conv_p.append(cst); kp_l.append(kn); vp_l.append(vn)
            xs, cst, kn, vn = even_layer(xs, c_sample, *pe, state_conv[i], cache_win_k[i], cache_win_v[i])
            conv_s.append(cst); ks_l.append(kn); vs_l.append(vn)
        else:
            po = (norm_o[i], ada_w_o[i], ada_b_o[i], w_in_o[i], pool_w[i], pool_scale[i], w_out_o[i])
            xp, pst, rst = odd_layer(xp, c_prompt, *po, None, None, 0)
            pool_p.append(pst); ret_p.append(rst)
            xs, pst, rst = odd_layer(xs, c_sample, *po, state_pool[i], state_ret[i], PAST_LEN)
            pool_s.append(pst); ret_s.append(rst)
    y_prompt = rmsnorm(xp, norm_f)
    y_sample = rmsnorm(xs, norm_f)
    return (y_prompt, y_sample, jnp.stack(conv_p), jnp.stack(conv_s), jnp.stack(kp_l), jnp.stack(ks_l),
            jnp.stack(vp_l), jnp.stack(vs_l), jnp.stack(pool_p), jnp.stack(pool_s), jnp.stack(ret_p), jnp.stack(ret_s))
```

```python
import contextlib
import numpy as np
import ml_dtypes
import concourse.bass as bass
import concourse.mybir as mybir
from concourse.bass_utils import run_bass_kernel_spmd

F32 = mybir.dt.float32
BF16 = mybir.dt.bfloat16
AF = mybir.ActivationFunctionType
ALU = mybir.AluOpType
AX = mybir.AxisListType

D = 1024
SEQ = 4096
NS = 4
NCORES = 8
EPS = 1e-6
WB = 2048
PATTERNS = ((128, 1), (512, 4), (2048, 16))
POOL_SIZES = (2, 4, 8, 16)
NEG = -1.0e9
LIMIT = None


class Res:
    __slots__ = ("name", "w", "rs", "psum")

    def __init__(self, name="", psum=False):
        self.name = name
        self.w = None
        self.rs = []
        self.psum = psum


class Ins:
    __slots__ = ("eng", "fn", "deps", "sig", "idx", "dma", "grp", "cnt")

    def __init__(self, eng, fn, dma=False):
        self.eng = eng
        self.fn = fn
        self.deps = []
        self.sig = False
        self.idx = None
        self.dma = dma
        self.grp = None
        self.cnt = None


class DmaGrp:
    def __init__(self, name):
        self.name = name
        self.sem = None
        self.count = 0
        self.last = None


class Sched:
    ENGS = ("pe", "act", "dve", "pool", "sp")

    def __init__(self, nc):
        self.nc = nc
        self.q = {e: [] for e in self.ENGS}
        self.last_c = {e: None for e in self.ENGS}
        self.pending = {e: [] for e in self.ENGS}
        self.grps = []
        self.out_grps = []

    def grp(self, name, out=False):
        g = DmaGrp(name + str(len(self.grps)))
        self.grps.append(g)
        if out:
            self.out_grps.append(g)
        return g

    def barrier(self):
        deps = [self.last_c[e] for e in self.ENGS if self.last_c[e] is not None]
        deps += [g.last for g in self.grps if g.last is not None]
        for e in self.ENGS:
            self.pending[e] = list(deps)

    def add(self, eng, fn, reads=(), writes=(), dma=None):
        self.nadd = getattr(self, "nadd", 0) + 1
        if LIMIT is not None and self.nadd > LIMIT:
            return None
        ins = Ins(eng, fn, dma=dma is not None)
        if self.pending[eng]:
            ins.deps.extend(self.pending[eng])
            self.pending[eng] = []
        if dma is not None:
            ins.grp = dma
            dma.count += 16
            ins.cnt = dma.count
            dma.last = ins
        else:
            self.last_c[eng] = ins
        for r in reads:
            if r.w is not None and r.w is not ins:
                ins.deps.append(r.w)
            if r.psum:
                for rd in r.rs:
                    if rd.eng != eng:
                        ins.deps.append(rd)
        def skip(o):
            return eng == "pe" and o.eng == "pe" and not o.dma and not ins.dma

        for w in writes:
            if w.w is not None and w.w is not ins and not skip(w.w):
                ins.deps.append(w.w)
            for rd in w.rs:
                if rd is not ins and not skip(rd):
                    ins.deps.append(rd)
        for r in reads:
            r.rs.append(ins)
        for w in writes:
            w.w = ins
            w.rs = []
        self.q[eng].append(ins)
        return ins

    def emit(self, final_eng="sp"):
        nc = self.nc
        for e in self.ENGS:
            for ins in self.q[e]:
                for d in ins.deps:
                    if not d.dma:
                        d.sig = True
        for e in self.ENGS:
            n = 0
            for ins in self.q[e]:
                if ins.sig and not ins.dma:
                    n += 1
                    ins.idx = n
        with contextlib.ExitStack() as st:
            esem = {e: st.enter_context(nc.semaphore("s_" + e)) for e in self.ENGS}
            for g in self.grps:
                g.sem = st.enter_context(nc.semaphore("g_" + g.name))
            block = st.enter_context(nc.Block())

            def replay(e, handle):
                waited = {}
                for ins in self.q[e]:
                    need = {}
                    for d in ins.deps:
                        if d.dma:
                            k, v = d.grp.sem, d.cnt
                        else:
                            k, v = esem[d.eng], d.idx
                        key = id(k)
                        if need.get(key, (None, 0))[1] < v:
                            need[key] = (k, v)
                    for key, (k, v) in need.items():
                        if waited.get(key, 0) < v:
                            handle.wait_ge(k, v)
                            waited[key] = v
                    bi = ins.fn(handle)
                    if ins.dma:
                        bi.then_inc(ins.grp.sem, 16)
                    elif ins.sig:
                        bi.then_inc(esem[e], 1)
                if e == final_eng:
                    for g in self.grps:
                        if g.count:
                            handle.wait_ge(g.sem, g.count)

            @block.tensor
            def _(h):
                replay("pe", h)

            @block.scalar
            def _(h):
                replay("act", h)

            @block.vector
            def _(h):
                replay("dve", h)

            @block.gpsimd
            def _(h):
                replay("pool", h)

            @block.sync
            def _(h):
                replay("sp", h)


def view(ap, shape):
    if len(shape) == 1:
        return ap
    names = ["d%d" % i for i in range(len(shape))]
    s = "p (" + " ".join(names) + ") -> p " + " ".join(names)
    kw = {names[i]: shape[i] for i in range(1, len(shape))}
    return ap.rearrange(s, **kw)


class Arena:
    def __init__(self, t, words):
        self.t = t
        self.words = words
        self.off = 0

    def alloc(self, shape, dt, parts=128):
        n = int(np.prod(shape))
        w = n if dt == F32 else (n + 1) // 2
        w = (w + 7) // 8 * 8
        assert self.off + w <= self.words, ("arena overflow", self.off, w, self.words)
        ap = self.t[0:parts, self.off:self.off + w]
        self.off += w
        if dt != F32:
            ap = ap.bitcast(dt)
        ap = ap[:, 0:n]
        return view(ap, list(shape))

    def mark(self):
        return self.off

    def release(self, m):
        self.off = m


def make_consts():
    c = {}
    c["ident_f"] = np.eye(128, dtype=np.float32)
    k = np.arange(128)[:, None]
    q = np.arange(128)[None, :]
    prev = np.where(q <= k, -(128.0 + q - k), NEG)
    cur = np.where(q >= k, -(q - k) * 1.0, NEG)
    masked = np.full((128, 128), NEG)
    nd = np.zeros((128, 2, 2, 2, 128), np.float32)
    nd[:, 0, 0, 0] = masked
    nd[:, 0, 0, 1] = cur
    nd[:, 0, 1, 0] = prev
    nd[:, 0, 1, 1] = cur
    nd[:, 1, :, 0] = prev[:, None, :]
    nd[:, 1, :, 1] = cur[:, None, :]
    c["nd"] = nd.reshape(128, 2, 512)
    nd2 = np.zeros((128, 2, 2, 128), np.float32)
    nd2[:, :, 0] = masked[:, None, :]
    nd2[:, :, 1] = cur[:, None, :]
    c["nd_ff"] = nd2.reshape(128, 512)
    g = 1.0 - 2.0 ** (-5.0 - np.arange(4))
    lg = np.log(g)
    i = np.arange(128)
    dec = np.zeros((128, 4, 128), np.float64)
    for h in range(4):
        df = i[None, :] - i[:, None]
        dec[:, h, :] = np.where(df >= 0, np.exp(np.maximum(df, 0) * lg[h]), 0.0)
    c["decT"] = dec.astype(np.float32)
    qs = np.zeros((128, 4, 128), np.float64)
    for h in range(4):
        qs[:, h, :] = np.exp((i + 1.0) * lg[h])[None, :]
    c["qsc"] = qs.astype(np.float32)
    kd = np.zeros((128, 4), np.float64)
    for h in range(4):
        kd[:, h] = np.exp((127.0 - i) * lg[h]) / 16.0
    c["kdec"] = kd.astype(np.float32)
    rc = np.zeros((128, 4, 16), np.float32)
    for gi, w in enumerate(POOL_SIZES):
        rc[:, gi, :] = (1.0 / np.minimum(float(w), np.arange(16) + 1.0))[None, :]
    c["rcp"] = rc
    slopes = 2.0 ** (-8.0 * np.arange(1, 17) / 16.0)
    db = np.zeros((128, 3, 16), np.float32)
    for pi, (w, d) in enumerate(PATTERNS):
        steps = 128.0 - np.arange(128)
        db[:, pi, :] = -(steps[:, None] * d) * slopes[None, :]
    c["dbias"] = db
    sel = np.zeros((4, 4, 128), np.float32)
    for s in range(4):
        sel[s, s, :] = 1.0
    c["sel4"] = sel
    eye = np.zeros((128, 4, 4), np.float32)
    for s in range(4):
        eye[:, s, s] = 1.0
    c["eye4"] = eye
    return c


CONST_SHAPES = {"ident_f": [128, 128], "nd": [128, 2, 512], "nd_ff": [128, 512], "decT": [128, 4, 128],
                "qsc": [128, 4, 128], "kdec": [128, 4], "rcp": [128, 4, 16], "dbias": [128, 3, 16],
                "sel4": [4, 4, 128], "eye4": [128, 4, 4]}

IN_SHAPES = {
    "xp": [SEQ, D], "xs": [NS, D], "cp": [1, D], "cs": [NS, D],
    "st_conv": [NS, 2, D], "ck": [NS, WB, D], "cv": [NS, WB, D], "st_pool": [NS, 15, D], "st_ret": [NS, 4, 256, 256],
    "norm_e": [D], "ada_w_e": [D, 3 * D], "ada_b_e": [3 * D], "w_in_e": [D, 8 * D], "conv_w": [3, D], "conv_b": [D],
    "w_out_e": [2 * D, D], "norm_o": [D], "ada_w_o": [D, 3 * D], "ada_b_o": [3 * D], "w_in_o": [D, 6 * D],
    "pool_w": [4, 256, 256], "pool_scale": [D], "w_out_o": [2 * D, D], "norm_f": [D],
}
OUT_SHAPES = {
    "y_p": [SEQ, D], "y_s": [NS, D], "conv_p": [2, D], "conv_s": [NS, 2, D], "k_p": [WB, D], "k_s": [NS, D],
    "v_p": [WB, D], "v_s": [NS, D], "pool_p": [15, D], "pool_s": [NS, 15, D], "ret_p": [4, 256, 256], "ret_s": [NS, 4, 256, 256],
}


def build_program(stage=99, debug=False):
    nc = bass.Bass("TRN2", target_bir_lowering=False)
    I = {k: nc.dram_tensor(k, v, F32, kind="ExternalInput").ap() for k, v in IN_SHAPES.items()}
    C = {k: nc.dram_tensor("c_" + k, v, F32, kind="ExternalInput").ap() for k, v in CONST_SHAPES.items()}
    O = {k: nc.dram_tensor(k, v, F32, kind="ExternalOutput").ap() for k, v in OUT_SHAPES.items()}
    Ysc = nc.dram_tensor("Ysc", [SEQ // 256, 128, 16, 256], BF16).ap()
    X1 = nc.dram_tensor("X1", [SEQ, D], F32).ap()
    MODS = nc.dram_tensor("MODS", [2, 3, NS, D], F32).ap()
    GATE = nc.dram_tensor("GATE", [2, D], F32).ap()
    X1S = nc.dram_tensor("X1S", [NS, D], F32).ap()
    DBG = {}
    if debug:
        DBG["x1"] = nc.dram_tensor("dbg_x1", [SEQ, D], F32, kind="ExternalOutput").ap()
        DBG["x1s"] = nc.dram_tensor("dbg_x1s", [NS, D], F32, kind="ExternalOutput").ap()

    S = Sched(nc)
    st = contextlib.ExitStack()
    WORDS = 52000
    arena_t = st.enter_context(nc.sbuf_tensor("arena", [128, WORDS], F32))
    A = Arena(arena_t, WORDS)
    banks = [st.enter_context(nc.psum_tensor("bank%d" % i, [128, 512], F32)) for i in range(8)]
    bank_res = [Res("bank%d" % i, psum=True) for i in range(8)]
    bank_ctr = [0]

    rot = [list(range(8))]

    def bank():
        r_ = rot[0]
        i = r_[bank_ctr[0] % len(r_)]
        bank_ctr[0] += 1
        return banks[i], bank_res[i]

    def mm(out, lhsT, rhs, start, stop, reads, writes):
        S.add("pe", lambda e: e.matmul(out, lhsT=lhsT, rhs=rhs, start=start, stop=stop), reads, writes)

    def tr(out, in_, ident, reads, writes):
        S.add("pe", lambda e: e.transpose(out, in_, ident), reads, writes)

    def act(out, in_, func, reads, writes, bias=None, scale=None):
        kw = {}
        if bias is not None:
            kw["bias"] = bias
        if scale is not None:
            kw["scale"] = scale
        S.add("act", lambda e: e.activation(out=out, in_=in_, func=func, **kw), reads, writes)

    def tt(eng, out, in0, in1, op, reads, writes):
        S.add(eng, lambda e: e.tensor_tensor(out=out, in0=in0, in1=in1, op=op), reads, writes)

    def ts(eng, out, in0, s1, s2, op0, op1, reads, writes):
        if s2 is None:
            S.add(eng, lambda e: e.tensor_scalar(out=out, in0=in0, scalar1=s1, scalar2=None, op0=op0), reads, writes)
        else:
            S.add(eng, lambda e: e.tensor_scalar(out=out, in0=in0, scalar1=s1, scalar2=s2, op0=op0, op1=op1), reads, writes)

    def stt(eng, out, in0, scalar, in1, op0, op1, reads, writes):
        S.add(eng, lambda e: e.scalar_tensor_tensor(out=out, in0=in0, scalar=scalar, in1=in1, op0=op0, op1=op1), reads, writes)

    def cp(eng, out, in_, reads, writes):
        if eng == "act":
            act(out, in_, AF.Copy, reads, writes)
        else:
            S.add(eng, lambda e: e.tensor_copy(out=out, in_=in_), reads, writes)

    def red(eng, out, in_, reads, writes):
        S.add(eng, lambda e: e.tensor_reduce(out=out, in_=in_, axis=AX.X, op=ALU.add), reads, writes)

    def recip(out, in_, reads, writes):
        S.add("dve", lambda e: e.reciprocal(out=out, in_=in_), reads, writes)

    def memset(eng, ap, val, writes):
        S.add(eng, lambda e: e.memset(ap, val), (), writes)

    def dma(q, out, in_, grp, reads=(), writes=(), nonc=False):
        if nonc:
            S.add(q, lambda e: e.dma_start(out=out, in_=in_, allow_slow_non_contiguous=True), reads, writes, dma=grp)
        else:
            S.add(q, lambda e: e.dma_start(out=out, in_=in_), reads, writes, dma=grp)

    def rstd_from(ap, n, r):
        act(ap, ap, AF.Ln, [r, cres], [r], bias=eps_t[0:ap.shape[0], 0:1], scale=1.0 / n)
        act(ap, ap, AF.Exp, [r], [r], scale=-0.5)

    hT = A.alloc([8, SEQ + NS], BF16)
    hT_r = [[Res("hT%d_%d" % (t, p)) for p in range(2)] for t in range(SEQ // 128)]
    hT_s = Res("hT_s")

    def hT_reads(t0, t1):
        return [hT_r[t][p] for t in range(t0, t1) for p in range(2)]

    g_in = S.grp("cin")
    cst = {}
    for k_, shp in CONST_SHAPES.items():
        cst[k_] = A.alloc(shp[1:], F32, parts=shp[0])
        dma("sp", cst[k_], C[k_], g_in, writes=[Res()])
    ident_b = A.alloc([128], BF16)
    ones_b = A.alloc([64], BF16)
    ones_f = A.alloc([128], F32)
    eps_t = A.alloc([1], F32)

    PB1 = A.alloc([128], F32, parts=80)
    PB2 = A.alloc([128], F32, parts=64)

    def rows(src):
        return src.rearrange("(j p) -> j p", p=128)

    pb_list = [(PB1, 0, I["norm_e"]), (PB1, 8, I["norm_o"]), (PB1, 16, I["ada_b_e"]), (PB1, 40, I["ada_b_o"]),
               (PB1, 64, I["conv_b"]), (PB1, 72, I["pool_scale"]),
               (PB2, 0, I["conv_w"][0]), (PB2, 8, I["conv_w"][1]), (PB2, 16, I["conv_w"][2]), (PB2, 24, I["cp"][0])]
    pb_list += [(PB2, 32 + 8 * s_, I["cs"][s_]) for s_ in range(NS)]
    for (dst, r0, src) in pb_list:
        nr = src.shape[0] // 128
        dma("sp", dst[r0:r0 + nr, :], rows(src), g_in, writes=[Res()])
    PT1 = A.alloc([80], F32)
    PT2 = A.alloc([64], F32)
    allin = Res("allin")
    allin.w = g_in.last
    cres = Res("c2")
    cp("dve", ident_b, cst["ident_f"], [allin], [cres])
    memset("dve", ones_b, 1.0, [cres])
    memset("dve", ones_f, 1.0, [cres])
    memset("dve", eps_t, EPS, [cres])
    pA_, rA_ = bank()
    tr(pA_[:, 0:80], PB1[0:80, :], cst["ident_f"][0:80, 0:80], [allin], [rA_])
    tr(pA_[:, 128:192], PB2[0:64, :], cst["ident_f"][0:64, 0:64], [allin], [rA_])
    cp("dve", PT1, pA_[:, 0:80], [rA_], [cres])
    cp("dve", PT2, pA_[:, 128:192], [rA_], [cres])
    pres = cres
    gT = [PT1[:, 0:8], PT1[:, 8:16]]
    abT = [PT1[:, 16:40], PT1[:, 40:64]]
    cbT = PT1[:, 64:72]
    psT = PT1[:, 72:80]
    cwT = PT2[:, 0:24].rearrange("p (j c) -> p j c", c=8)

    def load_rep(src, n, parts, grp, r):
        t = A.alloc([n], F32, parts=parts)
        dma("sp", t, src.partition_broadcast(parts), grp, writes=[r])
        return t

    gsc = A.alloc([2, 8], F32)
    shT = A.alloc([2, 8], F32)
    mres = Res("mod")
    ysT = A.alloc([16, NS], BF16)
    ysT_res = Res("ysT")
    g_out = S.grp("out", out=True)
    g_scr = S.grp("scr", out=True)
    g_mods = S.grp("mods", out=True)
    g_gate = S.grp("gate", out=True)
    cp_g = S.grp("cvp", out=True)
    sa_g = S.grp("sa", out=True)
    zs_g = S.grp("zs", out=True)

    class NormCtx:
        pass

    mods_dr = [Res("modsd0"), Res("modsd1")]

    def load_msh(n):
        dma("sp", n.msh, MODS[n.L].rearrange("a s n -> s a n"), S.grp("msh"), reads=[mods_dr[n.L]], writes=[n.msh_r])

    def alloc_norm(L, defer_msh=False):
        n = NormCtx()
        n.L = L
        n.xbuf = [A.alloc([D], F32) for _ in range(3)]
        n.xbuf_r = [Res("xbuf%d" % i) for i in range(3)]
        n.xbuf_g = [S.grp("xb") for _ in range(3)]
        n.sqj = A.alloc([D], F32)
        n.sqj_r = Res("sqj")
        n.ssq = [A.alloc([1], F32) for _ in range(3)]
        n.ssq_r = [Res("ssq%d" % i) for i in range(3)]
        n.ctr = 0
        n.s_t1 = A.alloc([D], F32, parts=4)
        n.s_ss = A.alloc([1], F32, parts=4)
        n.sn_r = Res("sn")
        n.g = S.grp("nrm")
        if L < 2:
            n.xn = [A.alloc([D], BF16) for _ in range(2)]
            n.xn_r = [Res("xn0"), Res("xn1")]
            n.s_hb = A.alloc([D], BF16, parts=4)
            n.msh = A.alloc([3, D], F32, parts=4)
            n.msh_r = Res("msh")
            if not defer_msh:
                load_msh(n)
        return n

    def norm_tile(n, ti, xt, xt_r):
        norm_post(n, ti, norm_pre(n, ti, xt, xt_r))

    def norm_pre(n, ti, xt, xt_r):
        L = n.L
        i3 = n.ctr % 3
        i2 = n.ctr % 2
        n.ctr += 1
        tt("dve", n.sqj, xt, xt, ALU.mult, [xt_r], [n.sqj_r])
        red("dve", n.ssq[i3], n.sqj, [n.sqj_r], [n.ssq_r[i3]])
        rstd_from(n.ssq[i3], D, n.ssq_r[i3])
        act(n.xn[i2], xt, AF.Copy, [xt_r, n.ssq_r[i3]], [n.xn_r[i2]], scale=n.ssq[i3][:, 0:1])
        return i2

    def norm_post(n, ti, i2):
        L = n.L
        pT, rT = bank()
        pTb = pT[:].bitcast(BF16)
        for j in range(8):
            tr(pTb[:, j * 128:(j + 1) * 128], n.xn[i2][:, j * 128:(j + 1) * 128], ident_b, [n.xn_r[i2], cres], [rT])
        for j in range(8):
            o = hT[:, j, ti * 128:(ti + 1) * 128]
            if ti % 2 == 0:
                act(o, pTb[:, j * 128:(j + 1) * 128], AF.Identity, [rT, mres], [hT_r[ti][j % 2]],
                    bias=shT[:, L, j:j + 1], scale=gsc[:, L, j:j + 1])
            else:
                ts("dve", o, pTb[:, j * 128:(j + 1) * 128], gsc[:, L, j:j + 1], shT[:, L, j:j + 1], ALU.mult, ALU.add,
                   [rT, mres], [hT_r[ti][j % 2]])

    def norm_sample(n, x4, x4_r):
        tt("dve", n.s_t1, x4, x4, ALU.mult, [x4_r], [n.sn_r])
        red("dve", n.s_ss, n.s_t1, [n.sn_r], [n.sn_r])
        rstd_from(n.s_ss, D, n.sn_r)
        stt("dve", n.s_t1, x4, n.s_ss[:, 0:1], n.msh[:, 1, :], ALU.mult, ALU.mult, [x4_r, n.sn_r, n.msh_r], [n.sn_r])
        tt("dve", n.s_hb, n.s_t1, n.msh[:, 0, :], ALU.add, [n.sn_r, n.msh_r], [n.sn_r])
        pT, rT = bank()
        pTb = pT[:].bitcast(BF16)
        for j in range(8):
            tr(pTb[:, j * NS:(j + 1) * NS], n.s_hb[:, j * 128:(j + 1) * 128], ident_b[0:NS, 0:NS], [n.sn_r, cres], [rT])
        cp("act", hT[:, :, SEQ:SEQ + NS], view(pTb[:, 0:8 * NS], [8, NS]), [rT], [hT_s])

    def sample_to_ysT(src_b, src_r, chunk0, nchunk):
        pT, rT = bank()
        pTb = pT[:].bitcast(BF16)
        for j in range(nchunk):
            tr(pTb[:, j * NS:(j + 1) * NS], src_b[:, j * 128:(j + 1) * 128], ident_b[0:NS, 0:NS], [src_r, cres], [rT])
        cp("act", ysT[:, chunk0:chunk0 + nchunk, :], view(pTb[:, 0:nchunk * NS], [nchunk, NS]), [rT], [ysT_res])

    pend_gens = []

    def step_pending():
        for gen in list(pend_gens):
            try:
                next(gen)
            except StopIteration:
                pend_gens.remove(gen)

    def flush_pending():
        while pend_gens:
            step_pending()

    def start_tail(gen):
        pend_gens.append(gen)
        try:
            next(gen)
        except StopIteration:
            pend_gens.remove(gen)

    m1 = A.mark()
    n0 = alloc_norm(0, defer_msh=True)
    xs4 = A.alloc([D], F32, parts=4)
    xs_res = Res("xs4")
    next_tile = [0]
    pend0 = []

    def emit_norm_tile():
        ti = next_tile[0]
        next_tile[0] += 1
        b = ti % 3
        dma("sp", n0.xbuf[b], I["xp"][ti * 128:(ti + 1) * 128, :], n0.xbuf_g[b], writes=[n0.xbuf_r[b]])
        pend0.append((ti, norm_pre(n0, ti, n0.xbuf[b], n0.xbuf_r[b])))
        if len(pend0) > 1:
            norm_post(n0, *pend0.pop(0))

    m0 = A.mark()
    scT = A.alloc([8, 1 + NS], F32)
    sres = Res("scT")
    act(scT, PT2[:, 24:64].rearrange("p (v k) -> p k v", k=8), AF.Silu, [pres], [sres])
    screp = A.alloc([8, 128], F32)
    cp("dve", screp, scT[:, :, 0:1].to_broadcast([128, 8, 128]), [sres], [sres])
    awp = [A.alloc([8, 512], F32) for _ in range(2)]
    awp_r = [Res("awp0"), Res("awp1")]
    awp_g = [S.grp("awp"), S.grp("awp")]
    ab4 = A.alloc([3 * D], F32, parts=4)
    abg = A.alloc([D], F32)
    g4 = A.alloc([D], F32, parts=4)
    tbl_r = Res("tbl")
    tbl_g = S.grp("tbl")
    gate_t = A.alloc([D], F32)
    gate_r = Res("gate_t")
    mod_s = A.alloc([3 * D], F32, parts=4)
    mods_r = Res("mod_s")
    pi_ = 0
    for L in range(2):
        aw = I["ada_w_e"] if L == 0 else I["ada_w_o"]
        ab = I["ada_b_e"] if L == 0 else I["ada_b_o"]
        gn = I["norm_e"] if L == 0 else I["norm_o"]
        dma("sp", ab4, ab.partition_broadcast(4), tbl_g, writes=[tbl_r])
        dma("sp", abg, ab[2 * D:3 * D].partition_broadcast(128), tbl_g, writes=[Res()])
        dma("sp", g4, gn.partition_broadcast(4), tbl_g, writes=[Res()])
        tbl_r.w = tbl_g.last
        awv = aw.rearrange("(k p) n -> p k n", p=128)
        rot[0] = list(range(7))
        psA, rA = banks[7], bank_res[7]
        for j in range(6):
            b = pi_ % 2
            pi_ += 1
            dma("sp", awp[b], awv[:, :, j * 512:(j + 1) * 512], awp_g[b], writes=[awp_r[b]])
            if L == 1 or j >= 4:
                for _ in range(4):
                    if next_tile[0] < SEQ // 128:
                        emit_norm_tile()
            pS_, rS = bank()
            for k in range(8):
                mm(pS_[0:NS, :], scT[:, k, 1:1 + NS], awp[b][:, k, :], k == 0, k == 7, [sres, awp_r[b]], [rS])
            tt("dve", mod_s[:, j * 512:(j + 1) * 512], pS_[0:NS, :], ab4[:, j * 512:(j + 1) * 512], ALU.add,
               [rS, tbl_r], [mods_r])
            if j < 4:
                for c4 in range(4):
                    cc = j * 4 + c4
                    for k in range(8):
                        mm(psA[:, cc:cc + 1], awp[b][:, k, c4 * 128:(c4 + 1) * 128], scT[:, k, 0:1], k == 0, k == 7,
                           [sres, awp_r[b]], [rA])
                if j == 3:
                    tt("dve", shT[:, L, :], psA[:, 0:8], abT[L][:, 0:8], ALU.add, [rA, pres], [mres])
                    tt("dve", gsc[:, L, :], psA[:, 8:16], abT[L][:, 8:16], ALU.add, [rA, pres], [mres])
                    stt("dve", gsc[:, L, :], gsc[:, L, :], 1.0, gT[L], ALU.add, ALU.mult, [mres, pres], [mres])
            else:
                pG, rG = bank()
                for k in range(8):
                    mm(pG[:, :], screp[:, k, :], awp[b][:, k, :], k == 0, k == 7, [sres, awp_r[b]], [rG])
                hh = j - 4
                tt("dve", gate_t[:, hh * 512:(hh + 1) * 512], pG[:, :], abg[:, hh * 512:(hh + 1) * 512], ALU.add,
                   [rG, tbl_r], [gate_r])
        stt("dve", mod_s[:, D:2 * D], mod_s[:, D:2 * D], 1.0, g4, ALU.add, ALU.mult, [mods_r, tbl_r], [mods_r])
        dma("sp", MODS[L].rearrange("a s n -> s a n"), view(mod_s, [3, D]), g_mods, reads=[mods_r], writes=[mods_dr[L]])
        dma("sp", GATE[L:L + 1, :], gate_t[0:1, :], g_gate, reads=[gate_r])
        if L == 0:
            load_msh(n0)
            dma("sp", xs4, I["xs"], n0.g, writes=[xs_res])
    rot[0] = list(range(8))
    while next_tile[0] < SEQ // 128:
        emit_norm_tile()
    while pend0:
        norm_post(n0, *pend0.pop(0))
    norm_sample(n0, xs4, xs_res)
    S.barrier()
    A.release(m1)

    NT = SEQ // 512
    wctr = [0]

    class Panels:
        pass

    def alloc_panels(ncols, nsec):
        P = Panels()
        P.buf = [A.alloc([8, ncols], BF16) for _ in range(2)]
        P.r = [[Res("wp%d_%d" % (b, s)) for s in range(nsec)] for b in range(2)]
        P.g = [S.grp("wp") for _ in range(2)]
        return P

    def load_panel(P, wv, cols, width):
        b = wctr[0] % 2
        wctr[0] += 1
        for s, c0 in enumerate(cols):
            dma("pool", P.buf[b][:, :, s * width:(s + 1) * width], wv[:, :, c0:c0 + width], P.g[b],
                writes=[P.r[b][0]] if s == 0 else [Res()])
        for r_ in P.r[b]:
            r_.w = P.g[b].last
            r_.rs = []
        P.r[b][0].rs = []
        return P.buf[b], P.r[b]

    class Prefetch:
        def __init__(self, P, wv, specs, width):
            self.P, self.wv, self.specs, self.width = P, wv, specs, width
            self.i = 0
            self.nxt = load_panel(P, wv, specs[0], width)

        def get(self):
            cur = self.nxt
            self.i += 1
            if self.i < len(self.specs):
                self.nxt = load_panel(self.P, self.wv, self.specs[self.i], self.width)
            return cur

    m2 = A.mark()
    w_in_v = I["w_in_e"].rearrange("(k p) n -> p k n", p=128)
    P0 = alloc_panels(512, 4)
    zs = A.alloc([512], F32, parts=4)
    zs_r = Res("zs")
    sgs = A.alloc([128], F32, parts=4)
    s_a = A.alloc([128], F32, parts=4)
    s_b = A.alloc([128], F32, parts=4)
    s_c = A.alloc([128], F32, parts=4)
    s_yb = A.alloc([128], BF16, parts=4)
    sa_r = Res("sa")

    def sample_proj(W, Wr, ncol=512, silu_cols=(384, 512), zs_=None, sg_=None):
        zs_ = zs if zs_ is None else zs_
        sg_ = sgs if sg_ is None else sg_
        for c0 in range(0, ncol, 512):
            pZ, rZ = bank()
            for k in range(8):
                mm(pZ[0:NS, :], hT[:, k, SEQ:SEQ + NS], W[:, k, c0:c0 + 512], k == 0, k == 7, [hT_s, Wr[0]], [rZ])
            cp("act", zs_[:, c0:c0 + 512], pZ[0:NS, :], [rZ], [zs_r])
            lo, hi = silu_cols
            if c0 <= lo and hi <= c0 + 512:
                act(sg_[:, 0:hi - lo], pZ[0:NS, lo - c0:hi - c0], AF.Silu, [rZ], [zs_r])

    mA = A.mark()
    pa_g = S.grp("pa")
    pa_r = Res("pa")
    cb4 = load_rep(I["conv_b"], D, 4, pa_g, Res())
    cw4 = A.alloc([3, D], F32, parts=4)
    for j in range(3):
        dma("sp", cw4[:, j, :], I["conv_w"][j].partition_broadcast(4), pa_g, writes=[Res()])
    stc = A.alloc([2, D], F32, parts=4)
    dma("sp", stc, I["st_conv"], pa_g, writes=[Res()])
    pa_r.w = pa_g.last
    ub = [A.alloc([514], F32) for _ in range(2)]
    ub_r = [Res("ub0"), Res("ub1")]
    cgs = [A.alloc([512], F32) for _ in range(2)]
    cgs_r = [Res("cgs0"), Res("cgs1")]
    sga = [A.alloc([512], F32) for _ in range(2)]
    sga_r = [Res("sga0"), Res("sga1")]
    t1b = [A.alloc([512], F32) for _ in range(2)]
    t1_r = [Res("t1_0"), Res("t1_1")]
    bgs = [A.alloc([512], F32) for _ in range(2)]
    bgs_r = [Res("bgs0"), Res("bgs1")]
    yab = [A.alloc([512], BF16) for _ in range(2)]
    ya_r = [Res("ya0"), Res("ya1")]
    ya_g = [S.grp("ya", out=True) for _ in range(2)]
    uctr = 0
    if stage >= 2:
        pfA = Prefetch(P0, w_in_v, [[s * D + g * 128 for s in range(4)] for g in range(8)], 128)
        for g in range(8):
            W, Wr = pfA.get()
            for tt_ in range(NT):
                i2 = uctr % 2
                uctr += 1
                u = ub[i2]
                un = ub[1 - i2]
                pb = []
                for s in range(4):
                    p_, r_ = bank()
                    for k in range(8):
                        mm(p_[:, :], W[:, k, s * 128:(s + 1) * 128], hT[:, k, tt_ * 512:(tt_ + 1) * 512], k == 0, k == 7,
                           [Wr[0]] + hT_reads(tt_ * 4, tt_ * 4 + 4), [r_])
                    pb.append((p_, r_))
                (p_bg, r_bg), (p_cg, r_cg), (p_xv, r_xv), (p_ga, r_ga) = pb
                step_pending()
                cp("act", cgs[i2], p_cg[:, :], [r_cg], [cgs_r[i2]])
                cp("act", bgs[i2], p_bg[:, :], [r_bg], [bgs_r[i2]])
                act(sga[i2], p_ga[:, :], AF.Silu, [r_ga], [sga_r[i2]])
                if tt_ == 0:
                    memset("pool", u[:, 0:2], 0.0, [ub_r[i2]])
                tt("dve", u[:, 2:514], cgs[i2], p_xv[:, :], ALU.mult, [cgs_r[i2], r_xv], [ub_r[i2]])
                if tt_ < NT - 1:
                    cp("pool", un[:, 0:2], u[:, 512:514], [ub_r[i2]], [ub_r[1 - i2]])
                ts("pool", t1b[i2], u[:, 2:514], cwT[:, 2, g:g + 1], cbT[:, g:g + 1], ALU.mult, ALU.add, [ub_r[i2], pres], [t1_r[i2]])
                stt("dve", t1b[i2], u[:, 1:513], cwT[:, 1, g:g + 1], t1b[i2], ALU.mult, ALU.add, [ub_r[i2], t1_r[i2], pres], [t1_r[i2]])
                stt("dve", t1b[i2], u[:, 0:512], cwT[:, 0, g:g + 1], t1b[i2], ALU.mult, ALU.add, [ub_r[i2], t1_r[i2], pres], [t1_r[i2]])
                tt("dve", t1b[i2], t1b[i2], bgs[i2], ALU.mult, [t1_r[i2], bgs_r[i2]], [t1_r[i2]])
                tt("dve", yab[i2], t1b[i2], sga[i2], ALU.mult, [t1_r[i2], sga_r[i2]], [ya_r[i2]])
                dma("sp", Ysc[2 * tt_:2 * tt_ + 2, :, g, :].rearrange("t p n -> p t n"), view(yab[i2], [2, 256]), ya_g[i2], reads=[ya_r[i2]])
                if tt_ == NT - 1:
                    dma("sp", O["conv_p"][:, g * 128:(g + 1) * 128].rearrange("r p -> p r"), u[:, 512:514], cp_g,
                        reads=[ub_r[i2]], nonc=True)
            def tailA(g, W, Wr):
                sample_proj(W, Wr)
                gc = slice(g * 128, (g + 1) * 128)
                tt("dve", s_a, zs[:, 128:256], zs[:, 256:384], ALU.mult, [zs_r], [sa_r])
                dma("sp", O["conv_s"][:, 1, gc], s_a, sa_g, reads=[sa_r])
                tt("dve", s_b, s_a, cw4[:, 2, gc], ALU.mult, [sa_r, pa_r], [sa_r])
                tt("dve", s_b, s_b, cb4[:, gc], ALU.add, [sa_r, pa_r], [sa_r])
                tt("dve", s_c, stc[:, 1, gc], cw4[:, 1, gc], ALU.mult, [pa_r], [sa_r])
                tt("dve", s_b, s_b, s_c, ALU.add, [sa_r], [sa_r])
                tt("dve", s_c, stc[:, 0, gc], cw4[:, 0, gc], ALU.mult, [pa_r], [sa_r])
                tt("dve", s_b, s_b, s_c, ALU.add, [sa_r], [sa_r])
                tt("dve", s_b, s_b, zs[:, 0:128], ALU.mult, [sa_r, zs_r], [sa_r])
                tt("dve", s_yb, s_b, sgs, ALU.mult, [sa_r, zs_r], [sa_r])
                yield
                yield
                sample_to_ysT(s_yb, sa_r, g, 1)
            start_tail(tailA(g, W, Wr))
        dma("sp", O["conv_s"][:, 0, :], I["st_conv"][:, 1, :], g_out)
    flush_pending()
    S.barrier()
    A.release(mA)

    qT = A.alloc([SEQ], BF16)
    kT = A.alloc([SEQ], BF16)
    gbs = A.alloc([SEQ], BF16)
    qkv_r = [[Res("q%d" % t), Res("k%d" % t), Res("v%d" % t), Res("gb%d" % t)] for t in range(NT)]
    Vblk = A.alloc([3, 32, 128], BF16)
    vb_r = [[Res("vb%d_%d" % (p, j)) for j in range(4)] for p in range(3)]
    kvo = [A.alloc([256], F32) for _ in range(2)]
    kvo_r = [Res("kvo0"), Res("kvo1")]
    kvo_g = [S.grp("kvo", out=True) for _ in range(2)]
    m_acc = A.mark()
    acc = A.alloc([2, WB], F32)
    acc_r = Res("acc")
    A.release(m_acc)
    vT = A.alloc([SEQ], BF16)
    A.release(m_acc)
    A.alloc([2, WB], F32)
    Kc = A.alloc([NS, 3, 128], F32)
    Vc = A.alloc([NS, 3, 2, 65], F32)
    kv_own = Res("kvown")
    memset("pool", Vc[:, :, :, :, 64:65], 1.0, [kv_own])
    Etab = A.alloc([2, 3, 2, 512], BF16)
    Etab_r = Res("Etab")
    pTt = [A.alloc([512], BF16) for _ in range(4)]
    pTt_r = [Res("pT%d" % i) for i in range(4)]
    ybt = A.alloc([WB], BF16)
    yb_r = Res("ybt")
    yb_g = S.grp("yb", out=True)
    kc_g = S.grp("kc")
    prod = A.alloc([NS, 128], F32)
    sc = A.alloc([NS, 3, 2], F32)
    pz = A.alloc([NS, 6, 4], F32)
    dr = Res("dec")
    sf = A.alloc([8], F32, parts=4)
    ot = A.alloc([128], F32, parts=4)
    slopes = [2.0 ** (-8.0 * (h + 1) / 16.0) for h in range(16)]
    ck_v = [I["ck"].rearrange("s (j d) c -> j d s c", d=d) for (_, d) in PATTERNS]
    cv_v = [I["cv"].rearrange("s (j d) c -> j d s c", d=d) for (_, d) in PATTERNS]
    sctr = 0
    if stage >= 3:
        pfB = Prefetch(P0, w_in_v, [[(4 + s) * D + g * 128 for s in range(4)] for g in range(8)], 128)
        for g in range(8):
            gc = slice(g * 128, (g + 1) * 128)
            W, Wr = pfB.get()
            for hh in range(2):
                for pi, (w_, d) in enumerate(PATTERNS):
                    c_ = float(slopes[2 * g + hh] * d)
                    if d == 1:
                        srcs = (cst["nd"][:, 0, :], cst["nd"][:, 1, :])
                    else:
                        srcs = (cst["nd_ff"], cst["nd"][:, 1, :])
                    for v_, src_ in enumerate(srcs):
                        act(Etab[:, hh, pi, v_, :], src_, AF.Exp, [cres], [Etab_r], scale=c_)
            for tt_ in range(NT):
                tk = slice(tt_ * 512, (tt_ + 1) * 512)
                for s in range(4):
                    p_, r_ = bank()
                    for k in range(8):
                        mm(p_[:, :], W[:, k, s * 128:(s + 1) * 128], hT[:, k, tk], k == 0, k == 7,
                           [Wr[0]] + hT_reads(tt_ * 4, tt_ * 4 + 4), [r_])
                    if s == 0:
                        act(qT[:, tk], p_[:, :], AF.Copy, [r_], [qkv_r[tt_][0]], scale=0.125)
                    elif s == 1:
                        cp("dve", kT[:, tk], p_[:, :], [r_], [qkv_r[tt_][1]])
                    elif s == 2:
                        cp("dve", vT[:, tk], p_[:, :], [r_], [acc_r])
                    else:
                        act(gbs[:, tk], p_[:, :], AF.Silu, [r_], [qkv_r[tt_][3]])
                step_pending()
            kvall_g = Res("kvall")
            for pi, (w_, d) in enumerate(PATTERNS):
                j0 = WB // d - 128
                dma("sp", Kc[:, :, pi, :], ck_v[pi][j0:j0 + 128, 0, :, gc], kc_g, writes=[kv_own] if pi == 0 else [Res()])
                for hh in range(2):
                    dma("sp", Vc[:, :, pi, hh, 0:64],
                        cv_v[pi][j0:j0 + 128, 0, :, g * 128 + hh * 64:g * 128 + (hh + 1) * 64], kc_g, writes=[Res()])
            kvall_g.w = kc_g.last
            kv_own.w = kc_g.last
            for t16 in range(16):
                ti = 16 + t16
                i2 = t16 % 2
                p_, r_ = bank()
                for k in range(8):
                    mm(p_[:, 0:256], hT[:, k, ti * 128:(ti + 1) * 128], W[:, k, 128:384], k == 0, k == 7,
                       [Wr[0]] + hT_reads(ti, ti + 1), [r_])
                cp("act", kvo[i2], p_[:, 0:256], [r_], [kvo_r[i2]])
                dma("sp", O["k_p"][t16 * 128:(t16 + 1) * 128, gc], kvo[i2][:, 0:128], kvo_g[i2], reads=[kvo_r[i2]])
                dma("sp", O["v_p"][t16 * 128:(t16 + 1) * 128, gc], kvo[i2][:, 128:256], kvo_g[i2], reads=[kvo_r[i2]])
            for pi, (w_, d) in enumerate(PATTERNS):
                vv = vT.rearrange("p (n i r) -> p n r i", i=128, r=d)
                for b8 in range(4):
                    pT, rT = bank()
                    pTb = pT[:].bitcast(BF16)
                    for j in range(8):
                        blk = b8 * 8 + j
                        n, r = blk // d, blk % d
                        tr(pTb[:, j * 128:(j + 1) * 128], vv[:, n, r, :], ident_b, [acc_r, cres], [rT])
                    cp("act" if b8 % 2 == 0 else "dve", Vblk[:, pi, b8 * 8:(b8 + 1) * 8, :], view(pTb[:, 0:1024], [8, 128]),
                       [rT], [vb_r[pi][b8]])
            allq = [qkv_r[t][0] for t in range(NT)]
            allk = [qkv_r[t][1] for t in range(NT)]
            for sp in range(2):
                jobs = []
                for pi, (w_, d) in enumerate(PATTERNS):
                    nper = 2048 // (128 * d)
                    if d == 1:
                        for n in range(0, 16, 2):
                            jobs.append((pi, d, (sp * 16 + n, 0), (sp * 16 + n + 1, 0)))
                    else:
                        for nl in range(nper):
                            for r in range(0, d, 2):
                                jobs.append((pi, d, (sp * nper + nl, r), (sp * nper + nl, r + 1)))
                units = [(j, hh) for j in range(len(jobs)) for hh in range(2)]
                qvs = [qT.rearrange("p (n i r) -> p n r i", i=128, r=d) for (_, d) in PATTERNS]
                kvs = [kT.rearrange("p (n i r) -> p n r i", i=128, r=d) for (_, d) in PATTERNS]
                accvs = [acc.rearrange("p o (n i r) -> p o n r i", i=128, r=d) for (_, d) in PATTERNS]

                def s_stage(j):
                    nonlocal sctr
                    pi, d, b0, b1 = jobs[j]
                    bk = [bank(), bank()]
                    for qi, (n, r) in enumerate((b0, b1)):
                        for role in range(2):
                            kn = n if (role == 1 or n == 0) else n - 1
                            col = (qi * 2 + role) * 128
                            for hh in range(2):
                                hp = slice(hh * 64, (hh + 1) * 64)
                                mm(bk[hh][0][:, col:col + 128], kvs[pi][hp, kn, r, :], qvs[pi][hp, n, r, :], True, True,
                                   allq + allk, [bk[hh][1]])
                    v_ = 0 if b0[0] == 0 else 1
                    bufs = []
                    for hh in range(2):
                        i4 = sctr % 4
                        sctr += 1
                        act(pTt[i4], bk[hh][0][:, :], AF.Exp, [bk[hh][1]], [pTt_r[i4]])
                        tt("dve" if hh == 0 else "pool", pTt[i4], pTt[i4], Etab[:, hh, pi, v_, :], ALU.mult,
                           [pTt_r[i4], Etab_r], [pTt_r[i4]])
                        bufs.append(i4)
                    return bufs

                def pv_stage(j, bufs, pOL, rOL):
                    pi, d, b0, b1 = jobs[j]
                    for qi, (n, r) in enumerate((b0, b1)):
                        for role in range(2):
                            kn = n if (role == 1 or n == 0) else n - 1
                            blk = kn * d + r
                            col = (qi * 2 + role) * 128
                            for hh in range(2):
                                hp = slice(hh * 64, (hh + 1) * 64)
                                i4 = bufs[hh]
                                mm(pOL[hp, qi * 128:(qi + 1) * 128], Vblk[:, pi, blk, hp], pTt[i4][:, col:col + 128],
                                   role == 0, role == 1, [vb_r[pi][blk // 8], pTt_r[i4]], [rOL])
                    for role in range(2):
                        for hh in range(2):
                            hp = slice(hh * 64, (hh + 1) * 64)
                            i4 = bufs[hh]
                            pv4 = view(pTt[i4], [2, 2, 128])
                            mm(pOL[hp, 256:512], ones_b, pv4[:, :, role, :], role == 0, role == 1, [cres, pTt_r[i4]], [rOL])

                def acc_stage(j, pOL, rOL):
                    pi, d, b0, b1 = jobs[j]
                    nper = 2048 // (128 * d)
                    (n0_, r0), (n1_, r1) = b0, b1
                    if d == 1:
                        nl = n0_ - sp * 16
                        dst = accvs[pi][:, :, nl:nl + 2, 0, :]
                    else:
                        nl = n0_ - sp * nper
                        dst = accvs[pi][:, :, nl, r0:r0 + 2, :]
                    src = view(pOL[:, :], [2, 2, 128])
                    if pi == 0:
                        cp("dve", dst, src, [rOL], [acc_r])
                    else:
                        tt("dve", dst, src, dst, ALU.add, [rOL, acc_r], [acc_r])

                nxt = s_stage(0)
                for j in range(len(jobs)):
                    bufs = nxt
                    if j + 1 < len(jobs):
                        nxt = s_stage(j + 1)
                    pOL, rOL = bank()
                    pv_stage(j, bufs, pOL, rOL)
                    acc_stage(j, pOL, rOL)
                tks = slice(sp * WB, (sp + 1) * WB)
                act(acc[:, 1, :], acc[:, 1, :], AF.Ln, [acc_r], [acc_r])
                act(acc[:, 1, :], acc[:, 1, :], AF.Exp, [acc_r], [acc_r], scale=-1.0)
                tt("pool", acc[:, 0, :], acc[:, 0, :], acc[:, 1, :], ALU.mult, [acc_r], [acc_r])
                tt("pool", ybt, acc[:, 0, :], gbs[:, tks], ALU.mult, [acc_r] + [qkv_r[t][3] for t in range(NT)], [yb_r])
                dma("sp", Ysc[8 * sp:8 * sp + 8, :, 8 + g, :].rearrange("t p n -> p t n"), view(ybt, [8, 256]), yb_g, reads=[yb_r])
            def tailB(g, W, Wr, kvall):
                gc = slice(g * 128, (g + 1) * 128)
                sample_proj(W, Wr)
                dma("sp", O["k_s"][:, gc], zs[:, 128:256], zs_g, reads=[zs_r])
                dma("sp", O["v_s"][:, gc], zs[:, 256:384], zs_g, reads=[zs_r])
                yield
                pQ, rQ = bank()
                for s in range(NS):
                    mm(pQ[:, s * 128:(s + 1) * 128], cst["sel4"][:, s, :], zs[:, 0:128], True, True, [cres, zs_r], [rQ])
                for pi in range(3):
                    tt("dve", prod, Kc[:, :, pi, :], view(pQ[:, :], [NS, 128]), ALU.mult, [kv_own, kvall, rQ], [dr])
                    red("dve", sc[:, :, pi, :], prod.rearrange("p s (h e) -> p s h e", e=64), [dr], [dr])
                    if pi == 1:
                        yield
                for pi in range(3):
                    stt("dve", sc[:, :, pi, :], sc[:, :, pi, :], 0.125,
                        cst["dbias"][:, pi, 2 * g:2 * g + 2].unsqueeze(1).to_broadcast([128, NS, 2]),
                        ALU.mult, ALU.add, [dr, cres], [dr])
                act(sc.rearrange("p s q h -> p (s q h)"), sc.rearrange("p s q h -> p (s q h)"), AF.Exp, [dr], [dr])
                for s in range(NS):
                    tt("dve", pz[:, s, :, :], sc[:, s, :, :].rearrange("p q h -> p (q h)").unsqueeze(2).to_broadcast([128, 6, 4]),
                       cst["eye4"][:, s, :].unsqueeze(1).to_broadcast([128, 6, 4]), ALU.mult, [dr, cres], [dr])
                tt("dve", s_a, zs[:, 0:128], zs[:, 128:256], ALU.mult, [zs_r], [sa_r])
                red("dve", sf[:, 0:2], s_a.rearrange("p (h e) -> p h e", e=64), [sa_r], [sa_r])
                act(sf[:, 0:2], sf[:, 0:2], AF.Exp, [sa_r], [sa_r], scale=0.125)
                ts("dve", sf[:, 0:2], sf[:, 0:2], 3.0, None, ALU.mult, None, [sa_r], [sa_r])
                yield
                pO, rO = bank()
                for hh in range(2):
                    first = True
                    for s in range(NS):
                        for pi in range(3):
                            last = (s == NS - 1 and pi == 2)
                            mm(pO[0:NS, hh * 65:(hh + 1) * 65], pz[:, s, pi * 2 + hh, :], Vc[:, s, pi, hh, :], first, last,
                               [dr, kv_own, kvall], [rO])
                            first = False
                yield
                for hh in range(2):
                    he = slice(hh * 64, (hh + 1) * 64)
                    stt("dve", ot[:, he], zs[:, 256 + hh * 64:256 + (hh + 1) * 64], sf[:, hh:hh + 1], pO[0:NS, hh * 65:hh * 65 + 64],
                        ALU.mult, ALU.add, [zs_r, sa_r, rO], [sa_r])
                    tt("dve", sf[:, 2 + hh:3 + hh], sf[:, hh:hh + 1], pO[0:NS, hh * 65 + 64:hh * 65 + 65], ALU.add, [sa_r, rO], [sa_r])
                recip(sf[:, 4:6], sf[:, 2:4], [sa_r], [sa_r])
                for hh in range(2):
                    he = slice(hh * 64, (hh + 1) * 64)
                    stt("dve", s_yb[:, he], ot[:, he], sf[:, 4 + hh:5 + hh], sgs[:, he], ALU.mult, ALU.mult, [sa_r, zs_r], [sa_r])
                yield
                sample_to_ysT(s_yb, sa_r, 8 + g, 1)
            start_tail(tailB(g, W, Wr, kvall_g))
    flush_pending()
    S.barrier()
    A.release(m2)

    def out_phase(L, w_out, x_src, final):
        m = A.mark()
        n = alloc_norm(1 if not final else 2)
        og = S.grp("og")
        og_r = Res("og")
        gate_rep = load_rep(GATE[L], D, 128, og, Res())
        gsm = A.alloc([3, D], F32, parts=4)
        dma("sp", gsm, MODS[L].rearrange("a s n -> s a n"), og, writes=[Res()])
        xin = A.alloc([D], F32, parts=4)
        dma("sp", xin, I["xs"] if L == 0 else X1S, og, writes=[Res()])
        if final:
            gf_rep = load_rep(I["norm_f"], D, 128, og, Res())
            gf4 = load_rep(I["norm_f"], D, 4, og, Res())
        og_r.w = og.last
        wo = A.alloc([16, D], BF16)
        wo_rk = [Res("wo%d" % q_) for q_ in range(4)]
        wov = w_out.rearrange("(k p) n -> p k n", p=128)
        for q_ in range(4):
            wo_g = S.grp("wo")
            for k in range(4 * q_, 4 * q_ + 4):
                dma("pool", wo[:, k, :], wov[:, k, :], wo_g, writes=[Res()])
            wo_rk[q_].w = wo_g.last
        yt = [A.alloc([16, 256], BF16) for _ in range(2)]
        yt_r = [Res("yt0"), Res("yt1")]
        yt_g = [S.grp("yt") for _ in range(2)]
        x1t = [A.alloc([D], F32) for _ in range(3)]
        x1t_r = [Res("x1t%d" % i) for i in range(3)]
        x1t_g = [S.grp("x1t", out=True) for _ in range(3)]
        if final:
            ob = [A.alloc([D], F32) for _ in range(2)]
            ob_r = [Res("ob0"), Res("ob1")]
            ob_g = [S.grp("ob", out=True) for _ in range(2)]
        pend = []

        def ld_y(t2_):
            dma("sp", yt[t2_ % 2], Ysc[t2_], yt_g[t2_ % 2], writes=[yt_r[t2_ % 2]])

        def ld_x(ti_):
            dma("sp", n.xbuf[ti_ % 3], x_src[ti_ * 128:(ti_ + 1) * 128, :], n.xbuf_g[ti_ % 3], writes=[n.xbuf_r[ti_ % 3]])

        ld_y(0)
        ld_x(0)
        ld_x(1)
        for t2 in range(SEQ // 256):
            b2 = t2 % 2
            if t2 + 1 < SEQ // 256:
                ld_y(t2 + 1)
            for sub in range(2):
                ti = t2 * 2 + sub
                b3 = ti % 3
                if ti + 2 < SEQ // 128:
                    ld_x(ti + 2)
                for half in range(2):
                    pX, rX = bank()
                    for k in range(16):
                        mm(pX[:, :], yt[b2][:, k, sub * 128:(sub + 1) * 128], wo[:, k, half * 512:(half + 1) * 512], k == 0, k == 15,
                           [yt_r[b2], wo_rk[k // 4]], [rX])
                    hs = slice(half * 512, (half + 1) * 512)
                    tt("dve", x1t[b3][:, hs], pX[:, :], gate_rep[:, hs], ALU.mult, [rX, og_r], [x1t_r[b3]])
                    tt("dve", x1t[b3][:, hs], x1t[b3][:, hs], n.xbuf[b3][:, hs], ALU.add, [x1t_r[b3], n.xbuf_r[b3]], [x1t_r[b3]])
                if not final:
                    dma("sp", X1[ti * 128:(ti + 1) * 128, :], x1t[b3], x1t_g[b3], reads=[x1t_r[b3]])
                    if debug:
                        dma("sp", DBG["x1"][ti * 128:(ti + 1) * 128, :], x1t[b3], x1t_g[b3], reads=[x1t_r[b3]])
                    pend.append((ti, norm_pre(n, ti, x1t[b3], x1t_r[b3])))
                    if len(pend) > 1:
                        norm_post(n, *pend.pop(0))
                else:
                    i3 = n.ctr % 3
                    n.ctr += 1
                    o2 = ti % 2
                    memset("pool", n.ssq[i3], 0.0, [n.ssq_r[i3]])
                    S.add("act", lambda e, o_=n.sqj, i_=x1t[b3], a_=n.ssq[i3]: e.activation(out=o_, in_=i_, func=AF.Square, accum_out=a_[:, 0:1]),
                          [x1t_r[b3], n.ssq_r[i3]], [n.sqj_r, n.ssq_r[i3]])
                    rstd_from(n.ssq[i3], D, n.ssq_r[i3])
                    stt("dve", ob[o2], x1t[b3], n.ssq[i3][:, 0:1], gf_rep, ALU.mult, ALU.mult,
                        [x1t_r[b3], n.ssq_r[i3], og_r], [ob_r[o2]])
                    dma("sp", O["y_p"][ti * 128:(ti + 1) * 128, :], ob[o2], ob_g[o2], reads=[ob_r[o2]])
        while pend:
            norm_post(n, *pend.pop(0))
        s_x = A.alloc([D], F32, parts=4)
        sx_r = Res("sx")
        for half in range(2):
            pX, rX = bank()
            hs = slice(half * 512, (half + 1) * 512)
            for k in range(16):
                mm(pX[0:NS, :], ysT[:, k, :], wo[:, k, hs], k == 0, k == 15, [ysT_res, wo_rk[k // 4]], [rX])
            tt("dve", s_x[:, hs], pX[0:NS, :], gsm[:, 2, hs], ALU.mult, [rX, og_r], [sx_r])
        tt("dve", s_x, s_x, xin, ALU.add, [sx_r, og_r], [sx_r])
        if not final:
            dma("sp", X1S, s_x, S.grp("x1s", out=True), reads=[sx_r])
            if debug:
                dma("sp", DBG["x1s"], s_x, g_out, reads=[sx_r])
            norm_sample(n, s_x, sx_r)
        else:
            tt("dve", n.s_t1, s_x, s_x, ALU.mult, [sx_r], [n.sn_r])
            red("dve", n.s_ss, n.s_t1, [n.sn_r], [n.sn_r])
            rstd_from(n.s_ss, D, n.sn_r)
            stt("dve", n.s_t1, s_x, n.s_ss[:, 0:1], gf4, ALU.mult, ALU.mult, [sx_r, n.sn_r, og_r], [n.sn_r])
            dma("sp", O["y_s"], n.s_t1, g_out, reads=[n.sn_r])
        S.barrier()
        A.release(m)

    if stage >= 4:
        out_phase(0, I["w_out_e"], I["xp"], False)

    if stage >= 5:
        m4 = A.mark()
        w1v = I["w_in_o"].rearrange("(k p) n -> p k n", p=128)
        gdec = [1.0 - 2.0 ** (-5.0 - h) for h in range(4)]
        pw_bf = A.alloc([4, 2, 256], BF16)
        pw_r = Res("pw")
        pw_g = S.grp("pw")
        for gi in range(4):
            dma("pool", pw_bf[:, gi, :, :], I["pool_w"][gi].rearrange("(c p) e -> p c e", p=128), pw_g, writes=[Res()])
        pw_r.w = pw_g.last
        ps4_r = Res("ps4")
        ps4 = load_rep(I["pool_scale"], D, 4, S.grp("ps4"), ps4_r)
        zs1 = A.alloc([1024], F32, parts=4)
        sg2 = A.alloc([256], F32, parts=4)
        s_w1 = A.alloc([256], F32, parts=4)
        s_w2 = A.alloc([256], F32, parts=4)
        s_y2 = A.alloc([256], BF16, parts=4)
        mC = A.mark()
        P1 = alloc_panels(512, 4)
        ue = [[A.alloc([528], F32) for _ in range(2)] for _ in range(2)]
        ue_r = [[Res("ue%d%d" % (h, p)) for p in range(2)] for h in range(2)]
        sA = [A.alloc([528], F32) for _ in range(2)]
        sB = [A.alloc([528], F32) for _ in range(2)]
        sw_r = [Res("sw0"), Res("sw1")]
        pld = [[A.alloc([512], BF16) for _ in range(2)] for _ in range(2)]
        pld_r = [[Res("pl%d%d" % (h, p)) for p in range(2)] for h in range(2)]
        sgc = [[A.alloc([512], F32) for _ in range(2)] for _ in range(2)]
        sgc_r = [[Res("sg%d%d" % (h, p)) for p in range(2)] for h in range(2)]
        ycb = [A.alloc([512], BF16) for _ in range(2)]
        yc_r = [Res("yc0"), Res("yc1")]
        yc_g = [S.grp("yc", out=True) for _ in range(2)]
        t16 = A.alloc([16], F32)
        t16_r = Res("t16")
        ppo = A.alloc([128], F32, parts=16)
        ppo_r = Res("ppo")
        ppo_g = S.grp("ppo", out=True)
        spt = A.alloc([15, 256], F32, parts=4)
        spt_r = Res("spt")
        spt_g = S.grp("spt")
        pls = A.alloc([2, NS], BF16)
        pls_r = Res("pls")
        ps_g = S.grp("pools", out=True)
        dma("sp", O["pool_s"][:, 0:14, :], I["st_pool"][:, 1:15, :], g_out)
        pfC = Prefetch(P1, w1v, [[gi * 256, gi * 256 + 128, D + gi * 256, D + gi * 256 + 128] for gi in range(4)], 128)
        pctr = 0
        yctr = 0
        pmix = None
        for gi in range(4):
            w_ = POOL_SIZES[gi]
            W, Wr = pfC.get()
            for tt_ in range(NT):
                i2 = pctr % 2
                pctr += 1
                tk = slice(tt_ * 512, (tt_ + 1) * 512)
                pb = []
                for s in range(4):
                    p_, r_ = bank()
                    for k in range(8):
                        mm(p_[:, :], W[:, k, s * 128:(s + 1) * 128], hT[:, k, tk], k == 0, k == 7,
                           [Wr[0]] + hT_reads(tt_ * 4, tt_ * 4 + 4), [r_])
                    pb.append((p_, r_))
                for hf in range(2):
                    u = ue[hf][i2]
                    ur = ue_r[hf][i2]
                    p_u, r_u = pb[hf]
                    p_g, r_g = pb[2 + hf]
                    cp("act", u[:, 16:528], p_u[:, :], [r_u], [ur])
                    act(sgc[hf][i2], p_g[:, :], AF.Silu, [r_g], [sgc_r[hf][i2]])
                    if tt_ == 0:
                        memset("pool", u[:, 0:16], 0.0, [ur])
                    if tt_ < NT - 1:
                        cp("pool", ue[hf][1 - i2][:, 0:16], u[:, 512:528], [ur], [ue_r[hf][1 - i2]])
                    a_, b_ = sA[hf], sB[hf]
                    we = "pool" if hf == 0 else "dve"
                    tt(we, a_[:, 1:528], u[:, 1:528], u[:, 0:527], ALU.add, [ur], [sw_r[hf]])
                    fin = a_
                    if gi >= 1:
                        tt(we, b_[:, 3:528], a_[:, 3:528], a_[:, 1:526], ALU.add, [sw_r[hf]], [sw_r[hf]])
                        fin = b_
                    if gi >= 2:
                        tt(we, a_[:, 7:528], b_[:, 7:528], b_[:, 3:524], ALU.add, [sw_r[hf]], [sw_r[hf]])
                        fin = a_
                    if gi >= 3:
                        tt(we, b_[:, 15:528], a_[:, 15:528], a_[:, 7:520], ALU.add, [sw_r[hf]], [sw_r[hf]])
                        fin = b_
                    stt("dve", pld[hf][i2], fin[:, 16:528], 1.0 / w_, u[:, 16:528], ALU.mult, ALU.subtract,
                        [sw_r[hf], ur], [pld_r[hf][i2]])
                    if tt_ == 0:
                        tt("dve", t16, fin[:, 16:32], cst["rcp"][:, gi, :], ALU.mult, [sw_r[hf], cres], [t16_r])
                        tt("dve", pld[hf][i2][:, 0:16], t16, u[:, 16:32], ALU.subtract, [t16_r, ur], [pld_r[hf][i2]])
                    if tt_ == NT - 1:
                        pT_, rT_ = bank()
                        tr(pT_[0:16, 0:128], u[:, 512:528], cst["ident_f"], [ur, cres], [rT_])
                        cp("dve", ppo, pT_[0:16, 0:128], [rT_], [ppo_r])
                        ch0 = gi * 256 + hf * 128
                        dma("sp", O["pool_p"][0:15, ch0:ch0 + 128], ppo[1:16, :], ppo_g, reads=[ppo_r])
                if pmix is not None:
                    pmix()
                step_pending()

                def mk(gi=gi, tt_=tt_, i2=i2):
                    def mix():
                        nonlocal yctr
                        for eh in range(2):
                            pM, rM = bank()
                            for c in range(2):
                                mm(pM[:, :], pw_bf[:, gi, c, eh * 128:(eh + 1) * 128], pld[c][i2], c == 0, c == 1, [pw_r, pld_r[c][i2]], [rM])
                            y2 = yctr % 2
                            yctr += 1
                            ch = 2 * gi + eh
                            stt("dve", ycb[y2], pM[:, :], psT[:, ch:ch + 1], sgc[eh][i2], ALU.mult, ALU.mult, [rM, pres, sgc_r[eh][i2]], [yc_r[y2]])
                            dma("sp", Ysc[2 * tt_:2 * tt_ + 2, :, ch, :].rearrange("t p n -> p t n"), view(ycb[y2], [2, 256]), yc_g[y2], reads=[yc_r[y2]])
                    return mix
                pmix = mk()
            def tailC(gi, w_, W, Wr):
                gcs = slice(gi * 256, (gi + 1) * 256)
                sample_proj(W, Wr, 512, (256, 512), zs1, sg2)
                dma("sp", O["pool_s"][:, 14, gcs], zs1[:, 0:256], ps_g, reads=[zs_r])
                dma("sp", spt, I["st_pool"][:, :, gcs], spt_g, writes=[spt_r])
                red("dve", s_w1, spt[:, 15 - (w_ - 1):15, :].rearrange("p r c -> p c r"), [spt_r], [sa_r])
                tt("dve", s_w1, s_w1, zs1[:, 0:256], ALU.add, [sa_r, zs_r], [sa_r])
                stt("dve", s_y2, s_w1, 1.0 / w_, zs1[:, 0:256], ALU.mult, ALU.subtract, [sa_r, zs_r], [sa_r])
                yield
                pT_, rT_ = bank()
                pTb_ = pT_[:].bitcast(BF16)
                for c in range(2):
                    tr(pTb_[:, c * NS:(c + 1) * NS], s_y2[:, c * 128:(c + 1) * 128], ident_b[0:NS, 0:NS], [sa_r, cres], [rT_])
                cp("act", pls, view(pTb_[:, 0:2 * NS], [2, NS]), [rT_], [pls_r])
                yield
                pMs, rMs = bank()
                for c in range(2):
                    mm(pMs[0:NS, 0:256], pls[:, c, :], pw_bf[:, gi, c, :], c == 0, c == 1, [pls_r, pw_r], [rMs])
                tt("dve", s_w2, pMs[0:NS, 0:256], ps4[:, gcs], ALU.mult, [rMs, ps4_r], [sa_r])
                tt("dve", s_y2, s_w2, sg2, ALU.mult, [sa_r, zs_r], [sa_r])
                yield
                sample_to_ysT(s_y2, sa_r, 2 * gi, 2)
            pmix()
            pmix = None
            start_tail(tailC(gi, w_, W, Wr))
        flush_pending()
        S.barrier()
        A.release(mC)

        PD = alloc_panels(1024, 4)
        Sm = A.alloc([2, 256], F32)
        Sb = A.alloc([2, 256], BF16)
        Sm_r = Res("Sm")
        Sb_r = Res("Sb")
        qTt = [A.alloc([2, 512], BF16) for _ in range(2)]
        qsT = [A.alloc([2, 512], BF16) for _ in range(2)]
        kTt = [A.alloc([2, 512], BF16) for _ in range(2)]
        gds = [A.alloc([2, 512], BF16) for _ in range(2)]
        q_r = [[Res("qt%d%d" % (p, c)) for c in range(2)] for p in range(2)]
        qs_r = [[Res("qs%d%d" % (p, c)) for c in range(2)] for p in range(2)]
        k_r = [[Res("kt%d%d" % (p, c)) for c in range(2)] for p in range(2)]
        gd_r = [[Res("gd%d%d" % (p, c)) for c in range(2)] for p in range(2)]
        ktm = [A.alloc([4, 256], BF16) for _ in range(2)]
        vtm = [A.alloc([4, 256], BF16) for _ in range(2)]
        kv_r = [[Res("kv%d%d" % (p, c)) for c in range(4)] for p in range(2)]
        ATb = [A.alloc([128], BF16) for _ in range(2)]
        AT_r = [Res("AT0"), Res("AT1")]
        sq = A.alloc([2, 512], F32)
        sq_r = Res("sq")
        rsd = A.alloc([512], F32)
        rsd_r = Res("rsd")
        ytmp = A.alloc([512], F32)
        ytmp_r = Res("ytmp")
        ydb = [A.alloc([512], BF16) for _ in range(2)]
        yd_r = [Res("yd0"), Res("yd1")]
        yd_g = [S.grp("yd", out=True) for _ in range(2)]
        rp_g = S.grp("retp", out=True)
        Sd = A.alloc([NS, 2, 256], F32)
        Sd_r = Res("Sd")
        Sd_g = S.grp("Sd")
        rs_g = S.grp("rets", out=True)
        ksel = A.alloc([NS, 256], F32, parts=4)
        ks_r = Res("ksel")
        qsel = A.alloc([2, NS, 4], F32)
        qsel_r = Res("qsel")
        o4s = A.alloc([256], F32, parts=4)
        pfD = Prefetch(PD, w1v, [[2 * D + hd * 256, 3 * D + hd * 256, 4 * D + hd * 256, 5 * D + hd * 256] for hd in range(4)], 256)
        rot[0] = [0, 1, 2, 3, 4, 5]
        pOt = [banks[6], banks[7]]
        pOt_r = [bank_res[6], bank_res[7]]
        tctr = 0
        actr = 0
        ydc = 0
        pfin = None
        for hd in range(4):
            W, Wr = pfD.get()
            g128 = float(gdec[hd] ** 128)
            memset("dve", Sm, 0.0, [Sm_r])
            memset("dve", Sb, 0.0, [Sb_r])
            for tt_ in range(NT):
                i2 = tctr % 2
                tctr += 1
                tk = slice(tt_ * 512, (tt_ + 1) * 512)
                hr = hT_reads(tt_ * 4, tt_ * 4 + 4)
                for sec, kind in ((0, "q"), (1, "k"), (3, "g")):
                    for c in range(2):
                        p_, r_ = bank()
                        c0 = sec * 256 + c * 128
                        for k in range(8):
                            mm(p_[:, :], W[:, k, c0:c0 + 128], hT[:, k, tk], k == 0, k == 7, [Wr[0]] + hr, [r_])
                        if kind == "q":
                            cp("act", qTt[i2][:, c, :], p_[:, :], [r_], [q_r[i2][c]])
                            tt("pool", view(qsT[i2][:, c, :], [4, 128]), view(qTt[i2][:, c, :], [4, 128]),
                               cst["qsc"][:, hd, :].unsqueeze(1).to_broadcast([128, 4, 128]), ALU.mult, [q_r[i2][c], cres], [qs_r[i2][c]])
                        elif kind == "k":
                            act(kTt[i2][:, c, :], p_[:, :], AF.Copy, [r_], [k_r[i2][c]], scale=1.0 / 16.0)
                        else:
                            act(gds[i2][:, c, :], p_[:, :], AF.Silu, [r_], [gd_r[i2][c]])
                step_pending()
                if pfin is not None:
                    pfin()
                    pfin = None
                for c4 in range(4):
                    ti = tt_ * 4 + c4
                    p_, r_ = bank()
                    for k in range(8):
                        mm(p_[:, :], hT[:, k, ti * 128:(ti + 1) * 128], W[:, k, 256:768], k == 0, k == 7, [Wr[0]] + hT_reads(ti, ti + 1), [r_])
                    ts("dve", ktm[i2][:, c4, :], p_[:, 0:256], cst["kdec"][:, hd:hd + 1], None, ALU.mult, None, [r_, cres], [kv_r[i2][c4]])
                    cp("dve", vtm[i2][:, c4, :], p_[:, 256:512], [r_], [kv_r[i2][c4]])
                def front(c4, i2=i2):
                    nonlocal actr
                    cs = slice(c4 * 128, (c4 + 1) * 128)
                    a2 = actr % 2
                    actr += 1
                    pSc, rSc = bank()
                    for dkc in range(2):
                        mm(pSc[:, 0:128], kTt[i2][:, dkc, cs], qTt[i2][:, dkc, cs], dkc == 0, dkc == 1, [k_r[i2][dkc], q_r[i2][dkc]], [rSc])
                    pSt, rSt = bank()
                    for dkc in range(2):
                        mm(pSt[:, dkc * 256:(dkc + 1) * 256], ktm[i2][:, c4, dkc * 128:(dkc + 1) * 128], vtm[i2][:, c4, :], True, True,
                           [kv_r[i2][c4]], [rSt])
                    tt("dve", ATb[a2], pSc[:, 0:128], cst["decT"][:, hd, :], ALU.mult, [rSc, cres], [AT_r[a2]])
                    return a2, pSt, rSt

                fr = front(0)
                for c4 in range(4):
                    cs = slice(c4 * 128, (c4 + 1) * 128)
                    a2, pSt, rSt = fr
                    if c4 + 1 < 4:
                        fr = front(c4 + 1)
                    for dvc in range(2):
                        dv = slice(dvc * 128, (dvc + 1) * 128)
                        mm(pOt[dvc][:, cs], vtm[i2][:, c4, dv], ATb[a2], True, False, [kv_r[i2][c4], AT_r[a2]], [pOt_r[dvc]])
                        mm(pOt[dvc][:, cs], Sb[:, 0, dv], qsT[i2][:, 0, cs], False, False, [Sb_r, qs_r[i2][0]], [pOt_r[dvc]])
                        mm(pOt[dvc][:, cs], Sb[:, 1, dv], qsT[i2][:, 1, cs], False, True, [Sb_r, qs_r[i2][1]], [pOt_r[dvc]])
                    stt("dve", Sm.rearrange("p c v -> p (c v)"), Sm.rearrange("p c v -> p (c v)"), g128, pSt[:, :],
                        ALU.mult, ALU.add, [Sm_r, rSt], [Sm_r])
                    cp("act", Sb, Sm, [Sm_r], [Sb_r])
                for dvc in range(2):
                    act(sq[:, dvc, :], pOt[dvc][:, :], AF.Square, [pOt_r[dvc]], [sq_r])

                def mk_fin(hd=hd, tt_=tt_, i2=i2):
                    def fin():
                        nonlocal ydc
                        pSS, rSS = bank()
                        for dvc in range(2):
                            mm(pSS[:, :], ones_f, sq[:, dvc, :], dvc == 0, dvc == 1, [cres, sq_r], [rSS])
                        act(rsd, pSS[:, :], AF.Ln, [rSS, cres], [rsd_r], bias=eps_t[:, 0:1], scale=1.0 / 256.0)
                        act(rsd, rsd, AF.Exp, [rsd_r], [rsd_r], scale=-0.5)
                        for dvc in range(2):
                            y2 = ydc % 2
                            ydc += 1
                            tt("dve", ytmp, pOt[dvc][:, :], rsd, ALU.mult, [pOt_r[dvc], rsd_r], [ytmp_r])
                            tt("dve", ydb[y2], ytmp, gds[i2][:, dvc, :], ALU.mult, [ytmp_r, gd_r[i2][dvc]], [yd_r[y2]])
                            dma("sp", Ysc[2 * tt_:2 * tt_ + 2, :, 8 + 2 * hd + dvc, :].rearrange("t p n -> p t n"), view(ydb[y2], [2, 256]),
                                yd_g[y2], reads=[yd_r[y2]])
                    return fin
                pfin = mk_fin()
            pfin()
            pfin = None
            dma("sp", O["ret_p"][hd].rearrange("(c p) v -> p c v", p=128), Sm, rp_g, reads=[Sm_r])
            def tailD(hd, W, Wr):
                sample_proj(W, Wr, 1024, (768, 1024), zs1, sg2)
                for s in range(NS):
                    dma("sp", Sd[:, s, :, :], I["st_ret"][s, hd].rearrange("(c p) v -> p c v", p=128), Sd_g,
                        writes=[Sd_r] if s == 0 else [Res()])
                Sd_r.w = Sd_g.last
                Sd_r.rs = []
                for s in range(NS):
                    ts("dve", ksel[:, s, :], zs1[:, 256:512], cst["ident_f"][0:NS, s:s + 1], 1.0 / 16.0, ALU.mult, ALU.mult, [zs_r, cres], [ks_r])
                yield
                for s in range(NS):
                    for dkc in range(2):
                        pSt, rSt = bank()
                        mm(pSt[:, 0:256], ksel[0:NS, s, dkc * 128:(dkc + 1) * 128], zs1[0:NS, 512:768], True, True, [ks_r, zs_r], [rSt])
                        stt("dve", Sd[:, s, dkc, :], Sd[:, s, dkc, :], float(gdec[hd]), pSt[:, 0:256], ALU.mult, ALU.add, [Sd_r, rSt], [Sd_r])
                    if s == 1:
                        yield
                for s in range(NS):
                    dma("sp", O["ret_s"][s, hd].rearrange("(c p) v -> p c v", p=128), Sd[:, s, :, :], rs_g, reads=[Sd_r])
                pTq, rTq = bank()
                for dkc in range(2):
                    tr(pTq[:, dkc * NS:(dkc + 1) * NS], zs1[0:NS, dkc * 128:(dkc + 1) * 128], cst["ident_f"][0:NS, 0:NS], [zs_r, cres], [rTq])
                for dkc in range(2):
                    tt("dve", qsel[:, dkc, :, :], pTq[:, dkc * NS:(dkc + 1) * NS].unsqueeze(2).to_broadcast([128, NS, 4]), cst["eye4"], ALU.mult,
                       [rTq, cres], [qsel_r])
                yield
                pOs, rOs = bank()
                n_ = 0
                for s in range(NS):
                    for dkc in range(2):
                        mm(pOs[0:NS, 0:256], qsel[:, dkc, s, :], Sd[:, s, dkc, :], n_ == 0, n_ == 2 * NS - 1, [qsel_r, Sd_r], [rOs])
                        n_ += 1
                cp("act", o4s, pOs[0:NS, 0:256], [rOs], [sa_r])
                tt("dve", s_w1, o4s, o4s, ALU.mult, [sa_r], [sa_r])
                red("dve", sf[:, 0:1], s_w1, [sa_r], [sa_r])
                rstd_from(sf[:, 0:1], 256, sa_r)
                stt("dve", s_y2, o4s, sf[:, 0:1], sg2, ALU.mult, ALU.mult, [sa_r, zs_r], [sa_r])
                yield
                sample_to_ysT(s_y2, sa_r, 8 + 2 * hd, 2)
            start_tail(tailD(hd, W, Wr))
        flush_pending()
        rot[0] = list(range(8))
        S.barrier()
        A.release(m4)
    if stage >= 6:
        out_phase(1, I["w_out_o"], X1, True)

    S.emit()
    st.close()
    return nc


_CACHE = {}
PROMPT_CORES = [0, 1, 4, 5]


def _prep_inputs(inputs):
    consts = make_consts()
    maps = []
    f = lambda a: np.ascontiguousarray(np.asarray(a, dtype=np.float32))
    shared = {
        "norm_e": f(inputs["norm_e"][0]), "ada_w_e": f(inputs["ada_w_e"][0]), "ada_b_e": f(inputs["ada_b_e"][0]),
        "w_in_e": f(inputs["w_in_e"][0]), "conv_w": f(inputs["conv_w"][0]), "conv_b": f(inputs["conv_b"][0]),
        "w_out_e": f(inputs["w_out_e"][0]), "norm_o": f(inputs["norm_o"][0]), "ada_w_o": f(inputs["ada_w_o"][0]),
        "ada_b_o": f(inputs["ada_b_o"][0]), "w_in_o": f(inputs["w_in_o"][0]), "pool_w": f(inputs["pool_w"][0]),
        "pool_scale": f(inputs["pool_scale"][0]), "w_out_o": f(inputs["w_out_o"][0]), "norm_f": f(inputs["norm_f"]),
    }
    for k, v in consts.items():
        shared["c_" + k] = f(v)
    zx = np.zeros((SEQ, D), np.float32)
    zc = np.zeros((1, D), np.float32)
    for c in range(NCORES):
        ss = slice(c * NS, (c + 1) * NS)
        m = dict(shared)
        if c in PROMPT_CORES:
            b = PROMPT_CORES.index(c)
            m["xp"] = f(inputs["x_prompt"][b])
            m["cp"] = f(inputs["c_prompt"][b:b + 1])
        else:
            m["xp"] = zx
            m["cp"] = zc
        m["xs"] = f(inputs["x_sample"][ss, 0])
        m["cs"] = f(inputs["c_sample"][ss])
        m["st_conv"] = f(inputs["state_conv"][0, ss])
        m["ck"] = f(np.asarray(inputs["cache_win_k"])[0, ss].reshape(NS, WB, D))
        m["cv"] = f(np.asarray(inputs["cache_win_v"])[0, ss].reshape(NS, WB, D))
        m["st_pool"] = f(inputs["state_pool"][0, ss])
        m["st_ret"] = f(inputs["state_ret"][0, ss])
        maps.append(m)
    return maps


def _run(inputs, stage=99, debug=False):
    key = (stage, debug)
    if key not in _CACHE:
        _CACHE[key] = build_program(stage, debug)
    nc = _CACHE[key]
    maps = _prep_inputs(inputs)
    res = run_bass_kernel_spmd(nc, maps, core_ids=list(range(NCORES)))
    return res.results


def kernel(**inputs):
    r = _run(inputs)
    g = lambda name, c: np.asarray(r[c][name], dtype=np.float32)
    y_p = np.stack([g("y_p", b) for b in PROMPT_CORES])
    y_s = np.concatenate([g("y_s", c) for c in range(NCORES)])[:, None, :]
    conv_p = np.stack([g("conv_p", b) for b in PROMPT_CORES])[None]
    conv_s = np.concatenate([g("conv_s", c) for c in range(NCORES)])[None]
    k_p = np.stack([g("k_p", b).reshape(WB, 16, 64) for b in PROMPT_CORES])[None]
    k_s = np.concatenate([g("k_s", c) for c in range(NCORES)]).reshape(1, 32, 1, 16, 64)
    v_p = np.stack([g("v_p", b).reshape(WB, 16, 64) for b in PROMPT_CORES])[None]
    v_s = np.concatenate([g("v_s", c) for c in range(NCORES)]).reshape(1, 32, 1, 16, 64)
    pool_p = np.stack([g("pool_p", b) for b in PROMPT_CORES])[None]
    pool_s = np.concatenate([g("pool_s", c) for c in range(NCORES)])[None]
    ret_p = np.stack([g("ret_p", b) for b in PROMPT_CORES])[None]
    ret_s = np.concatenate([g("ret_s", c) for c in range(NCORES)])[None]
    return (y_p, y_s, conv_p, conv_s, k_p, k_s, v_p, v_s, pool_p, pool_s, ret_p, ret_s)
```

```python
import contextlib
import numpy as np
import ml_dtypes
import concourse.bass as bass
import concourse.mybir as mybir
from concourse.bass_utils import run_bass_kernel_spmd

F32 = mybir.dt.float32
BF16 = mybir.dt.bfloat16
AF = mybir.ActivationFunctionType
ALU = mybir.AluOpType
AX = mybir.AxisListType

D = 1024
SEQ = 4096
NS = 4
NCORES = 8
EPS = 1e-6
WB = 2048
PATTERNS = ((128, 1), (512, 4), (2048, 16))
POOL_SIZES = (2, 4, 8, 16)
NEG = -1.0e9
LIMIT = None


class Res:
    __slots__ = ("name", "w", "rs", "psum")

    def __init__(self, name="", psum=False):
        self.name = name
        self.w = None
        self.rs = []
        self.psum = psum


class Ins:
    __slots__ = ("eng", "fn", "deps", "sig", "idx", "dma", "grp", "cnt")

    def __init__(self, eng, fn, dma=False):
        self.eng = eng
        self.fn = fn
        self.deps = []
        self.sig = False
        self.idx = None
        self.dma = dma
        self.grp = None
        self.cnt = None


class DmaGrp:
    def __init__(self, name):
        self.name = name
        self.sem = None
        self.count = 0
        self.last = None


class Sched:
    ENGS = ("pe", "act", "dve", "pool", "sp")

    def __init__(self, nc):
        self.nc = nc
        self.q = {e: [] for e in self.ENGS}
        self.last_c = {e: None for e in self.ENGS}
        self.pending = {e: [] for e in self.ENGS}
        self.grps = []
        self.out_grps = []

    def grp(self, name, out=False):
        g = DmaGrp(name + str(len(self.grps)))
        self.grps.append(g)
        if out:
            self.out_grps.append(g)
        return g

    def barrier(self):
        deps = [self.last_c[e] for e in self.ENGS if self.last_c[e] is not None]
        deps += [g.last for g in self.grps if g.last is not None]
        for e in self.ENGS:
            self.pending[e] = list(deps)

    def add(self, eng, fn, reads=(), writes=(), dma=None):
        self.nadd = getattr(self, "nadd", 0) + 1
        if LIMIT is not None and self.nadd > LIMIT:
            return None
        ins = Ins(eng, fn, dma=dma is not None)
        if self.pending[eng]:
            ins.deps.extend(self.pending[eng])
            self.pending[eng] = []
        if dma is not None:
            ins.grp = dma
            dma.count += 16
            ins.cnt = dma.count
            dma.last = ins
        else:
            self.last_c[eng] = ins
        for r in reads:
            if r.w is not None and r.w is not ins:
                ins.deps.append(r.w)
            if r.psum:
                for rd in r.rs:
                    if rd.eng != eng:
                        ins.deps.append(rd)
        def skip(o):
            return eng == "pe" and o.eng == "pe" and not o.dma and not ins.dma

        for w in writes:
            if w.w is not None and w.w is not ins and not skip(w.w):
                ins.deps.append(w.w)
            for rd in w.rs:
                if rd is not ins and not skip(rd):
                    ins.deps.append(rd)
        for r in reads:
            r.rs.append(ins)
        for w in writes:
            w.w = ins
            w.rs = []
        self.q[eng].append(ins)
        return ins

    def emit(self, final_eng="sp"):
        nc = self.nc
        for e in self.ENGS:
            for ins in self.q[e]:
                for d in ins.deps:
                    if not d.dma:
                        d.sig = True
        for e in self.ENGS:
            n = 0
            for ins in self.q[e]:
                if ins.sig and not ins.dma:
                    n += 1
                    ins.idx = n
        with contextlib.ExitStack() as st:
            esem = {e: st.enter_context(nc.semaphore("s_" + e)) for e in self.ENGS}
            for g in self.grps:
                g.sem = st.enter_context(nc.semaphore("g_" + g.name))
            block = st.enter_context(nc.Block())

            def replay(e, handle):
                waited = {}
                for ins in self.q[e]:
                    need = {}
                    for d in ins.deps:
                        if d.dma:
                            k, v = d.grp.sem, d.cnt
                        else:
                            k, v = esem[d.eng], d.idx
                        key = id(k)
                        if need.get(key, (None, 0))[1] < v:
                            need[key] = (k, v)
                    for key, (k, v) in need.items():
                        if waited.get(key, 0) < v:
                            handle.wait_ge(k, v)
                            waited[key] = v
                    bi = ins.fn(handle)
                    if ins.dma:
                        bi.then_inc(ins.grp.sem, 16)
                    elif ins.sig:
                        bi.then_inc(esem[e], 1)
                if e == final_eng:
                    for g in self.grps:
                        if g.count:
                            handle.wait_ge(g.sem, g.count)

            @block.tensor
            def _(h):
                replay("pe", h)

            @block.scalar
            def _(h):
                replay("act", h)

            @block.vector
            def _(h):
                replay("dve", h)

            @block.gpsimd
            def _(h):
                replay("pool", h)

            @block.sync
            def _(h):
                replay("sp", h)


def view(ap, shape):
    if len(shape) == 1:
        return ap
    names = ["d%d" % i for i in range(len(shape))]
    s = "p (" + " ".join(names) + ") -> p " + " ".join(names)
    kw = {names[i]: shape[i] for i in range(1, len(shape))}
    return ap.rearrange(s, **kw)


class Arena:
    def __init__(self, t, words):
        self.t = t
        self.words = words
        self.off = 0

    def alloc(self, shape, dt, parts=128):
        n = int(np.prod(shape))
        w = n if dt == F32 else (n + 1) // 2
        w = (w + 7) // 8 * 8
        assert self.off + w <= self.words, ("arena overflow", self.off, w, self.words)
        ap = self.t[0:parts, self.off:self.off + w]
        self.off += w
        if dt != F32:
            ap = ap.bitcast(dt)
        ap = ap[:, 0:n]
        return view(ap, list(shape))

    def mark(self):
        return self.off

    def release(self, m):
        self.off = m


def make_consts():
    c = {}
    c["ident_f"] = np.eye(128, dtype=np.float32)
    k = np.arange(128)[:, None]
    q = np.arange(128)[None, :]
    prev = np.where(q <= k, -(128.0 + q - k), NEG)
    cur = np.where(q >= k, -(q - k) * 1.0, NEG)
    masked = np.full((128, 128), NEG)
    nd = np.zeros((128, 2, 2, 2, 128), np.float32)
    nd[:, 0, 0, 0] = masked
    nd[:, 0, 0, 1] = cur
    nd[:, 0, 1, 0] = prev
    nd[:, 0, 1, 1] = cur
    nd[:, 1, :, 0] = prev[:, None, :]
    nd[:, 1, :, 1] = cur[:, None, :]
    c["nd"] = nd.reshape(128, 2, 512)
    nd2 = np.zeros((128, 2, 2, 128), np.float32)
    nd2[:, :, 0] = masked[:, None, :]
    nd2[:, :, 1] = cur[:, None, :]
    c["nd_ff"] = nd2.reshape(128, 512)
    g = 1.0 - 2.0 ** (-5.0 - np.arange(4))
    lg = np.log(g)
    i = np.arange(128)
    dec = np.zeros((128, 4, 128), np.float64)
    for h in range(4):
        df = i[None, :] - i[:, None]
        dec[:, h, :] = np.where(df >= 0, np.exp(np.maximum(df, 0) * lg[h]), 0.0)
    c["decT"] = dec.astype(np.float32)
    qs = np.zeros((128, 4, 128), np.float64)
    for h in range(4):
        qs[:, h, :] = np.exp((i + 1.0) * lg[h])[None, :]
    c["qsc"] = qs.astype(np.float32)
    kd = np.zeros((128, 4), np.float64)
    for h in range(4):
        kd[:, h] = np.exp((127.0 - i) * lg[h]) / 16.0
    c["kdec"] = kd.astype(np.float32)
    rc = np.zeros((128, 4, 16), np.float32)
    for gi, w in enumerate(POOL_SIZES):
        rc[:, gi, :] = (1.0 / np.minimum(float(w), np.arange(16) + 1.0))[None, :]
    c["rcp"] = rc
    slopes = 2.0 ** (-8.0 * np.arange(1, 17) / 16.0)
    db = np.zeros((128, 3, 16), np.float32)
    for pi, (w, d) in enumerate(PATTERNS):
        steps = 128.0 - np.arange(128)
        db[:, pi, :] = -(steps[:, None] * d) * slopes[None, :]
    c["dbias"] = db
    sel = np.zeros((4, 4, 128), np.float32)
    for s in range(4):
        sel[s, s, :] = 1.0
    c["sel4"] = sel
    eye = np.zeros((128, 4, 4), np.float32)
    for s in range(4):
        eye[:, s, s] = 1.0
    c["eye4"] = eye
    return c


CONST_SHAPES = {"ident_f": [128, 128], "nd": [128, 2, 512], "nd_ff": [128, 512], "decT": [128, 4, 128],
                "qsc": [128, 4, 128], "kdec": [128, 4], "rcp": [128, 4, 16], "dbias": [128, 3, 16],
                "sel4": [4, 4, 128], "eye4": [128, 4, 4]}

IN_SHAPES = {
    "xp": [SEQ, D], "xs": [NS, D], "cp": [1, D], "cs": [NS, D],
    "st_conv": [NS, 2, D], "ck": [NS, WB, D], "cv": [NS, WB, D], "st_pool": [NS, 15, D], "st_ret": [NS, 4, 256, 256],
    "norm_e": [D], "ada_w_e": [D, 3 * D], "ada_b_e": [3 * D], "w_in_e": [D, 8 * D], "conv_w": [3, D], "conv_b": [D],
    "w_out_e": [2 * D, D], "norm_o": [D], "ada_w_o": [D, 3 * D], "ada_b_o": [3 * D], "w_in_o": [D, 6 * D],
    "pool_w": [4, 256, 256], "pool_scale": [D], "w_out_o": [2 * D, D], "norm_f": [D],
}
OUT_SHAPES = {
    "y_p": [SEQ, D], "y_s": [NS, D], "conv_p": [2, D], "conv_s": [NS, 2, D], "k_p": [WB, D], "k_s": [NS, D],
    "v_p": [WB, D], "v_s": [NS, D], "pool_p": [15, D], "pool_s": [NS, 15, D], "ret_p": [4, 256, 256], "ret_s": [NS, 4, 256, 256],
}


def build_program(stage=99, debug=False):
    nc = bass.Bass("TRN2", target_bir_lowering=False)
    I = {k: nc.dram_tensor(k, v, F32, kind="ExternalInput").ap() for k, v in IN_SHAPES.items()}
    C = {k: nc.dram_tensor("c_" + k, v, F32, kind="ExternalInput").ap() for k, v in CONST_SHAPES.items()}
    O = {k: nc.dram_tensor(k, v, F32, kind="ExternalOutput").ap() for k, v in OUT_SHAPES.items()}
    Ysc = nc.dram_tensor("Ysc", [SEQ // 256, 128, 16, 256], BF16).ap()
    X1 = nc.dram_tensor("X1", [SEQ, D], F32).ap()
    MODS = nc.dram_tensor("MODS", [2, 3, NS, D], F32).ap()
    GATE = nc.dram_tensor("GATE", [2, D], F32).ap()
    X1S = nc.dram_tensor("X1S", [NS, D], F32).ap()
    DBG = {}
    if debug:
        DBG["x1"] = nc.dram_tensor("dbg_x1", [SEQ, D], F32, kind="ExternalOutput").ap()
        DBG["x1s"] = nc.dram_tensor("dbg_x1s", [NS, D], F32, kind="ExternalOutput").ap()

    S = Sched(nc)
    st = contextlib.ExitStack()
    WORDS = 52000
    arena_t = st.enter_context(nc.sbuf_tensor("arena", [128, WORDS], F32))
    A = Arena(arena_t, WORDS)
    banks = [st.enter_context(nc.psum_tensor("bank%d" % i, [128, 512], F32)) for i in range(8)]
    bank_res = [Res("bank%d" % i, psum=True) for i in range(8)]
    bank_ctr = [0]

    rot = [list(range(8))]

    def bank():
        r_ = rot[0]
        i = r_[bank_ctr[0] % len(r_)]
        bank_ctr[0] += 1
        return banks[i], bank_res[i]

    def mm(out, lhsT, rhs, start, stop, reads, writes):
        S.add("pe", lambda e: e.matmul(out, lhsT=lhsT, rhs=rhs, start=start, stop=stop), reads, writes)

    def tr(out, in_, ident, reads, writes):
        S.add("pe", lambda e: e.transpose(out, in_, ident), reads, writes)

    def act(out, in_, func, reads, writes, bias=None, scale=None):
        kw = {}
        if bias is not None:
            kw["bias"] = bias
        if scale is not None:
            kw["scale"] = scale
        S.add("act", lambda e: e.activation(out=out, in_=in_, func=func, **kw), reads, writes)

    def tt(eng, out, in0, in1, op, reads, writes):
        S.add(eng, lambda e: e.tensor_tensor(out=out, in0=in0, in1=in1, op=op), reads, writes)

    def ts(eng, out, in0, s1, s2, op0, op1, reads, writes):
        if s2 is None:
            S.add(eng, lambda e: e.tensor_scalar(out=out, in0=in0, scalar1=s1, scalar2=None, op0=op0), reads, writes)
        else:
            S.add(eng, lambda e: e.tensor_scalar(out=out, in0=in0, scalar1=s1, scalar2=s2, op0=op0, op1=op1), reads, writes)

    def stt(eng, out, in0, scalar, in1, op0, op1, reads, writes):
        S.add(eng, lambda e: e.scalar_tensor_tensor(out=out, in0=in0, scalar=scalar, in1=in1, op0=op0, op1=op1), reads, writes)

    def cp(eng, out, in_, reads, writes):
        if eng == "act":
            act(out, in_, AF.Copy, reads, writes)
        else:
            S.add(eng, lambda e: e.tensor_copy(out=out, in_=in_), reads, writes)

    def red(eng, out, in_, reads, writes):
        S.add(eng, lambda e: e.tensor_reduce(out=out, in_=in_, axis=AX.X, op=ALU.add), reads, writes)

    def recip(out, in_, reads, writes):
        S.add("dve", lambda e: e.reciprocal(out=out, in_=in_), reads, writes)

    def memset(eng, ap, val, writes):
        S.add(eng, lambda e: e.memset(ap, val), (), writes)

    def dma(q, out, in_, grp, reads=(), writes=(), nonc=False):
        if nonc:
            S.add(q, lambda e: e.dma_start(out=out, in_=in_, allow_slow_non_contiguous=True), reads, writes, dma=grp)
        else:
            S.add(q, lambda e: e.dma_start(out=out, in_=in_), reads, writes, dma=grp)

    def rstd_from(ap, n, r):
        act(ap, ap, AF.Ln, [r, cres], [r], bias=eps_t[0:ap.shape[0], 0:1], scale=1.0 / n)
        act(ap, ap, AF.Exp, [r], [r], scale=-0.5)

    hT = A.alloc([8, SEQ + NS], BF16)
    hT_r = [[Res("hT%d_%d" % (t, p)) for p in range(2)] for t in range(SEQ // 128)]
    hT_s = Res("hT_s")

    def hT_reads(t0, t1):
        return [hT_r[t][p] for t in range(t0, t1) for p in range(2)]

    g_in = S.grp("cin")
    cst = {}
    for k_, shp in CONST_SHAPES.items():
        cst[k_] = A.alloc(shp[1:], F32, parts=shp[0])
        dma("sp", cst[k_], C[k_], g_in, writes=[Res()])
    ident_b = A.alloc([128], BF16)
    ones_b = A.alloc([64], BF16)
    ones_f = A.alloc([128], F32)
    eps_t = A.alloc([1], F32)

    PB1 = A.alloc([128], F32, parts=80)
    PB2 = A.alloc([128], F32, parts=64)

    def rows(src):
        return src.rearrange("(j p) -> j p", p=128)

    pb_list = [(PB1, 0, I["norm_e"]), (PB1, 8, I["norm_o"]), (PB1, 16, I["ada_b_e"]), (PB1, 40, I["ada_b_o"]),
               (PB1, 64, I["conv_b"]), (PB1, 72, I["pool_scale"]),
               (PB2, 0, I["conv_w"][0]), (PB2, 8, I["conv_w"][1]), (PB2, 16, I["conv_w"][2]), (PB2, 24, I["cp"][0])]
    pb_list += [(PB2, 32 + 8 * s_, I["cs"][s_]) for s_ in range(NS)]
    for (dst, r0, src) in pb_list:
        nr = src.shape[0] // 128
        dma("sp", dst[r0:r0 + nr, :], rows(src), g_in, writes=[Res()])
    PT1 = A.alloc([80], F32)
    PT2 = A.alloc([64], F32)
    allin = Res("allin")
    allin.w = g_in.last
    cres = Res("c2")
    cp("dve", ident_b, cst["ident_f"], [allin], [cres])
    memset("dve", ones_b, 1.0, [cres])
    memset("dve", ones_f, 1.0, [cres])
    memset("dve", eps_t, EPS, [cres])
    pA_, rA_ = bank()
    tr(pA_[:, 0:80], PB1[0:80, :], cst["ident_f"][0:80, 0:80], [allin], [rA_])
    tr(pA_[:, 128:192], PB2[0:64, :], cst["ident_f"][0:64, 0:64], [allin], [rA_])
    cp("dve", PT1, pA_[:, 0:80], [rA_], [cres])
    cp("dve", PT2, pA_[:, 128:192], [rA_], [cres])
    pres = cres
    gT = [PT1[:, 0:8], PT1[:, 8:16]]
    abT = [PT1[:, 16:40], PT1[:, 40:64]]
    cbT = PT1[:, 64:72]
    psT = PT1[:, 72:80]
    cwT = PT2[:, 0:24].rearrange("p (j c) -> p j c", c=8)

    def load_rep(src, n, parts, grp, r):
        t = A.alloc([n], F32, parts=parts)
        dma("sp", t, src.partition_broadcast(parts), grp, writes=[r])
        return t

    gsc = A.alloc([2, 8], F32)
    shT = A.alloc([2, 8], F32)
    mres = Res("mod")
    ysT = A.alloc([16, NS], BF16)
    ysT_res = Res("ysT")
    g_out = S.grp("out", out=True)
    g_scr = S.grp("scr", out=True)
    g_mods = S.grp("mods", out=True)
    g_gate = S.grp("gate", out=True)
    cp_g = S.grp("cvp", out=True)
    sa_g = S.grp("sa", out=True)
    zs_g = S.grp("zs", out=True)

    class NormCtx:
        pass

    mods_dr = [Res("modsd0"), Res("modsd1")]

    def load_msh(n):
        dma("sp", n.msh, MODS[n.L].rearrange("a s n -> s a n"), S.grp("msh"), reads=[mods_dr[n.L]], writes=[n.msh_r])

    def alloc_norm(L, defer_msh=False):
        n = NormCtx()
        n.L = L
        n.xbuf = [A.alloc([D], F32) for _ in range(3)]
        n.xbuf_r = [Res("xbuf%d" % i) for i in range(3)]
        n.xbuf_g = [S.grp("xb") for _ in range(3)]
        n.sqj = A.alloc([D], F32)
        n.sqj_r = Res("sqj")
        n.ssq = [A.alloc([1], F32) for _ in range(3)]
        n.ssq_r = [Res("ssq%d" % i) for i in range(3)]
        n.ctr = 0
        n.s_t1 = A.alloc([D], F32, parts=4)
        n.s_ss = A.alloc([1], F32, parts=4)
        n.sn_r = Res("sn")
        n.g = S.grp("nrm")
        if L < 2:
            n.xn = [A.alloc([D], BF16) for _ in range(2)]
            n.xn_r = [Res("xn0"), Res("xn1")]
            n.s_hb = A.alloc([D], BF16, parts=4)
            n.msh = A.alloc([3, D], F32, parts=4)
            n.msh_r = Res("msh")
            if not defer_msh:
                load_msh(n)
        return n

    def norm_tile(n, ti, xt, xt_r):
        norm_post(n, ti, norm_pre(n, ti, xt, xt_r))

    def norm_pre(n, ti, xt, xt_r):
        L = n.L
        i3 = n.ctr % 3
        i2 = n.ctr % 2
        n.ctr += 1
        tt("dve", n.sqj, xt, xt, ALU.mult, [xt_r], [n.sqj_r])
        red("dve", n.ssq[i3], n.sqj, [n.sqj_r], [n.ssq_r[i3]])
        rstd_from(n.ssq[i3], D, n.ssq_r[i3])
        act(n.xn[i2], xt, AF.Copy, [xt_r, n.ssq_r[i3]], [n.xn_r[i2]], scale=n.ssq[i3][:, 0:1])
        return i2

    def norm_post(n, ti, i2):
        L = n.L
        pT, rT = bank()
        pTb = pT[:].bitcast(BF16)
        for j in range(8):
            tr(pTb[:, j * 128:(j + 1) * 128], n.xn[i2][:, j * 128:(j + 1) * 128], ident_b, [n.xn_r[i2], cres], [rT])
        for j in range(8):
            o = hT[:, j, ti * 128:(ti + 1) * 128]
            if ti % 2 == 0:
                act(o, pTb[:, j * 128:(j + 1) * 128], AF.Identity, [rT, mres], [hT_r[ti][j % 2]],
                    bias=shT[:, L, j:j + 1], scale=gsc[:, L, j:j + 1])
            else:
                ts("dve", o, pTb[:, j * 128:(j + 1) * 128], gsc[:, L, j:j + 1], shT[:, L, j:j + 1], ALU.mult, ALU.add,
                   [rT, mres], [hT_r[ti][j % 2]])

    def norm_sample(n, x4, x4_r):
        tt("dve", n.s_t1, x4, x4, ALU.mult, [x4_r], [n.sn_r])
        red("dve", n.s_ss, n.s_t1, [n.sn_r], [n.sn_r])
        rstd_from(n.s_ss, D, n.sn_r)
        stt("dve", n.s_t1, x4, n.s_ss[:, 0:1], n.msh[:, 1, :], ALU.mult, ALU.mult, [x4_r, n.sn_r, n.msh_r], [n.sn_r])
        tt("dve", n.s_hb, n.s_t1, n.msh[:, 0, :], ALU.add, [n.sn_r, n.msh_r], [n.sn_r])
        pT, rT = bank()
        pTb = pT[:].bitcast(BF16)
        for j in range(8):
            tr(pTb[:, j * NS:(j + 1) * NS], n.s_hb[:, j * 128:(j + 1) * 128], ident_b[0:NS, 0:NS], [n.sn_r, cres], [rT])
        cp("act", hT[:, :, SEQ:SEQ + NS], view(pTb[:, 0:8 * NS], [8, NS]), [rT], [hT_s])

    def sample_to_ysT(src_b, src_r, chunk0, nchunk):
        pT, rT = bank()
        pTb = pT[:].bitcast(BF16)
        for j in range(nchunk):
            tr(pTb[:, j * NS:(j + 1) * NS], src_b[:, j * 128:(j + 1) * 128], ident_b[0:NS, 0:NS], [src_r, cres], [rT])
        cp("act", ysT[:, chunk0:chunk0 + nchunk, :], view(pTb[:, 0:nchunk * NS], [nchunk, NS]), [rT], [ysT_res])

    pend_gens = []

    def step_pending():
        for gen in list(pend_gens):
            try:
                next(gen)
            except StopIteration:
                pend_gens.remove(gen)

    def flush_pending():
        while pend_gens:
            step_pending()

    def start_tail(gen):
        pend_gens.append(gen)
        try:
            next(gen)
        except StopIteration:
            pend_gens.remove(gen)

    m1 = A.mark()
    n0 = alloc_norm(0, defer_msh=True)
    xs4 = A.alloc([D], F32, parts=4)
    xs_res = Res("xs4")
    next_tile = [0]
    pend0 = []

    def emit_norm_tile():
        ti = next_tile[0]
        next_tile[0] += 1
        b = ti % 3
        dma("sp", n0.xbuf[b], I["xp"][ti * 128:(ti + 1) * 128, :], n0.xbuf_g[b], writes=[n0.xbuf_r[b]])
        pend0.append((ti, norm_pre(n0, ti, n0.xbuf[b], n0.xbuf_r[b])))
        if len(pend0) > 1:
            norm_post(n0, *pend0.pop(0))

    m0 = A.mark()
    scT = A.alloc([8, 1 + NS], F32)
    sres = Res("scT")
    act(scT, PT2[:, 24:64].rearrange("p (v k) -> p k v", k=8), AF.Silu, [pres], [sres])
    screp = A.alloc([8, 128], F32)
    cp("dve", screp, scT[:, :, 0:1].to_broadcast([128, 8, 128]), [sres], [sres])
    awp = [A.alloc([8, 512], F32) for _ in range(2)]
    awp_r = [Res("awp0"), Res("awp1")]
    awp_g = [S.grp("awp"), S.grp("awp")]
    ab4 = A.alloc([3 * D], F32, parts=4)
    abg = A.alloc([D], F32)
    g4 = A.alloc([D], F32, parts=4)
    tbl_r = Res("tbl")
    tbl_g = S.grp("tbl")
    gate_t = A.alloc([D], F32)
    gate_r = Res("gate_t")
    mod_s = A.alloc([3 * D], F32, parts=4)
    mods_r = Res("mod_s")
    pi_ = 0
    for L in range(2):
        aw = I["ada_w_e"] if L == 0 else I["ada_w_o"]
        ab = I["ada_b_e"] if L == 0 else I["ada_b_o"]
        gn = I["norm_e"] if L == 0 else I["norm_o"]
        dma("sp", ab4, ab.partition_broadcast(4), tbl_g, writes=[tbl_r])
        dma("sp", abg, ab[2 * D:3 * D].partition_broadcast(128), tbl_g, writes=[Res()])
        dma("sp", g4, gn.partition_broadcast(4), tbl_g, writes=[Res()])
        tbl_r.w = tbl_g.last
        awv = aw.rearrange("(k p) n -> p k n", p=128)
        rot[0] = list(range(7))
        psA, rA = banks[7], bank_res[7]
        for j in range(6):
            b = pi_ % 2
            pi_ += 1
            dma("sp", awp[b], awv[:, :, j * 512:(j + 1) * 512], awp_g[b], writes=[awp_r[b]])
            if L == 1 or j >= 4:
                for _ in range(4):
                    if next_tile[0] < SEQ // 128:
                        emit_norm_tile()
            pS_, rS = bank()
            for k in range(8):
                mm(pS_[0:NS, :], scT[:, k, 1:1 + NS], awp[b][:, k, :], k == 0, k == 7, [sres, awp_r[b]], [rS])
            tt("dve", mod_s[:, j * 512:(j + 1) * 512], pS_[0:NS, :], ab4[:, j * 512:(j + 1) * 512], ALU.add,
               [rS, tbl_r], [mods_r])
            if j < 4:
                for c4 in range(4):
                    cc = j * 4 + c4
                    for k in range(8):
                        mm(psA[:, cc:cc + 1], awp[b][:, k, c4 * 128:(c4 + 1) * 128], scT[:, k, 0:1], k == 0, k == 7,
                           [sres, awp_r[b]], [rA])
                if j == 3:
                    tt("dve", shT[:, L, :], psA[:, 0:8], abT[L][:, 0:8], ALU.add, [rA, pres], [mres])
                    tt("dve", gsc[:, L, :], psA[:, 8:16], abT[L][:, 8:16], ALU.add, [rA, pres], [mres])
                    stt("dve", gsc[:, L, :], gsc[:, L, :], 1.0, gT[L], ALU.add, ALU.mult, [mres, pres], [mres])
            else:
                pG, rG = bank()
                for k in range(8):
                    mm(pG[:, :], screp[:, k, :], awp[b][:, k, :], k == 0, k == 7, [sres, awp_r[b]], [rG])
                hh = j - 4
                tt("dve", gate_t[:, hh * 512:(hh + 1) * 512], pG[:, :], abg[:, hh * 512:(hh + 1) * 512], ALU.add,
                   [rG, tbl_r], [gate_r])
        stt("dve", mod_s[:, D:2 * D], mod_s[:, D:2 * D], 1.0, g4, ALU.add, ALU.mult, [mods_r, tbl_r], [mods_r])
        dma("sp", MODS[L].rearrange("a s n -> s a n"), view(mod_s, [3, D]), g_mods, reads=[mods_r], writes=[mods_dr[L]])
        dma("sp", GATE[L:L + 1, :], gate_t[0:1, :], g_gate, reads=[gate_r])
        if L == 0:
            load_msh(n0)
            dma("sp", xs4, I["xs"], n0.g, writes=[xs_res])
    rot[0] = list(range(8))
    while next_tile[0] < SEQ // 128:
        emit_norm_tile()
    while pend0:
        norm_post(n0, *pend0.pop(0))
    norm_sample(n0, xs4, xs_res)
    S.barrier()
    A.release(m1)

    NT = SEQ // 512
    wctr = [0]

    class Panels:
        pass

    def alloc_panels(ncols, nsec):
        P = Panels()
        P.buf = [A.alloc([8, ncols], BF16) for _ in range(2)]
        P.r = [[Res("wp%d_%d" % (b, s)) for s in range(nsec)] for b in range(2)]
        P.g = [S.grp("wp") for _ in range(2)]
        return P

    def load_panel(P, wv, cols, width):
        b = wctr[0] % 2
        wctr[0] += 1
        for s, c0 in enumerate(cols):
            dma("pool", P.buf[b][:, :, s * width:(s + 1) * width], wv[:, :, c0:c0 + width], P.g[b],
                writes=[P.r[b][0]] if s == 0 else [Res()])
        for r_ in P.r[b]:
            r_.w = P.g[b].last
            r_.rs = []
        P.r[b][0].rs = []
        return P.buf[b], P.r[b]

    class Prefetch:
        def __init__(self, P, wv, specs, width):
            self.P, self.wv, self.specs, self.width = P, wv, specs, width
            self.i = 0
            self.nxt = load_panel(P, wv, specs[0], width)

        def get(self):
            cur = self.nxt
            self.i += 1
            if self.i < len(self.specs):
                self.nxt = load_panel(self.P, self.wv, self.specs[self.i], self.width)
            return cur

    m2 = A.mark()
    w_in_v = I["w_in_e"].rearrange("(k p) n -> p k n", p=128)
    P0 = alloc_panels(512, 4)
    zs = A.alloc([512], F32, parts=4)
    zs_r = Res("zs")
    sgs = A.alloc([128], F32, parts=4)
    s_a = A.alloc([128], F32, parts=4)
    s_b = A.alloc([128], F32, parts=4)
    s_c = A.alloc([128], F32, parts=4)
    s_yb = A.alloc([128], BF16, parts=4)
    sa_r = Res("sa")

    def sample_proj(W, Wr, ncol=512, silu_cols=(384, 512), zs_=None, sg_=None):
        zs_ = zs if zs_ is None else zs_
        sg_ = sgs if sg_ is None else sg_
        for c0 in range(0, ncol, 512):
            pZ, rZ = bank()
            for k in range(8):
                mm(pZ[0:NS, :], hT[:, k, SEQ:SEQ + NS], W[:, k, c0:c0 + 512], k == 0, k == 7, [hT_s, Wr[0]], [rZ])
            cp("act", zs_[:, c0:c0 + 512], pZ[0:NS, :], [rZ], [zs_r])
            lo, hi = silu_cols
            if c0 <= lo and hi <= c0 + 512:
                act(sg_[:, 0:hi - lo], pZ[0:NS, lo - c0:hi - c0], AF.Silu, [rZ], [zs_r])

    mA = A.mark()
    pa_g = S.grp("pa")
    pa_r = Res("pa")
    cb4 = load_rep(I["conv_b"], D, 4, pa_g, Res())
    cw4 = A.alloc([3, D], F32, parts=4)
    for j in range(3):
        dma("sp", cw4[:, j, :], I["conv_w"][j].partition_broadcast(4), pa_g, writes=[Res()])
    stc = A.alloc([2, D], F32, parts=4)
    dma("sp", stc, I["st_conv"], pa_g, writes=[Res()])
    pa_r.w = pa_g.last
    ub = [A.alloc([514], F32) for _ in range(2)]
    ub_r = [Res("ub0"), Res("ub1")]
    cgs = [A.alloc([512], F32) for _ in range(2)]
    cgs_r = [Res("cgs0"), Res("cgs1")]
    sga = [A.alloc([512], F32) for _ in range(2)]
    sga_r = [Res("sga0"), Res("sga1")]
    t1b = [A.alloc([512], F32) for _ in range(2)]
    t1_r = [Res("t1_0"), Res("t1_1")]
    bgs = [A.alloc([512], F32) for _ in range(2)]
    bgs_r = [Res("bgs0"), Res("bgs1")]
    yab = [A.alloc([512], BF16) for _ in range(2)]
    ya_r = [Res("ya0"), Res("ya1")]
    ya_g = [S.grp("ya", out=True) for _ in range(2)]
    uctr = 0
    if stage >= 2:
        pfA = Prefetch(P0, w_in_v, [[s * D + g * 128 for s in range(4)] for g in range(8)], 128)
        for g in range(8):
            W, Wr = pfA.get()
            for tt_ in range(NT):
                i2 = uctr % 2
                uctr += 1
                u = ub[i2]
                un = ub[1 - i2]
                pb = []
                for s in range(4):
                    p_, r_ = bank()
                    for k in range(8):
                        mm(p_[:, :], W[:, k, s * 128:(s + 1) * 128], hT[:, k, tt_ * 512:(tt_ + 1) * 512], k == 0, k == 7,
                           [Wr[0]] + hT_reads(tt_ * 4, tt_ * 4 + 4), [r_])
                    pb.append((p_, r_))
                (p_bg, r_bg), (p_cg, r_cg), (p_xv, r_xv), (p_ga, r_ga) = pb
                step_pending()
                cp("act", cgs[i2], p_cg[:, :], [r_cg], [cgs_r[i2]])
                cp("act", bgs[i2], p_bg[:, :], [r_bg], [bgs_r[i2]])
                act(sga[i2], p_ga[:, :], AF.Silu, [r_ga], [sga_r[i2]])
                if tt_ == 0:
                    memset("pool", u[:, 0:2], 0.0, [ub_r[i2]])
                tt("dve", u[:, 2:514], cgs[i2], p_xv[:, :], ALU.mult, [cgs_r[i2], r_xv], [ub_r[i2]])
                if tt_ < NT - 1:
                    cp("pool", un[:, 0:2], u[:, 512:514], [ub_r[i2]], [ub_r[1 - i2]])
                ts("pool", t1b[i2], u[:, 2:514], cwT[:, 2, g:g + 1], cbT[:, g:g + 1], ALU.mult, ALU.add, [ub_r[i2], pres], [t1_r[i2]])
                stt("dve", t1b[i2], u[:, 1:513], cwT[:, 1, g:g + 1], t1b[i2], ALU.mult, ALU.add, [ub_r[i2], t1_r[i2], pres], [t1_r[i2]])
                stt("dve", t1b[i2], u[:, 0:512], cwT[:, 0, g:g + 1], t1b[i2], ALU.mult, ALU.add, [ub_r[i2], t1_r[i2], pres], [t1_r[i2]])
                tt("dve", t1b[i2], t1b[i2], bgs[i2], ALU.mult, [t1_r[i2], bgs_r[i2]], [t1_r[i2]])
                tt("dve", yab[i2], t1b[i2], sga[i2], ALU.mult, [t1_r[i2], sga_r[i2]], [ya_r[i2]])
                dma("sp", Ysc[2 * tt_:2 * tt_ + 2, :, g, :].rearrange("t p n -> p t n"), view(yab[i2], [2, 256]), ya_g[i2], reads=[ya_r[i2]])
                if tt_ == NT - 1:
                    dma("sp", O["conv_p"][:, g * 128:(g + 1) * 128].rearrange("r p -> p r"), u[:, 512:514], cp_g,
                        reads=[ub_r[i2]], nonc=True)
            def tailA(g, W, Wr):
                sample_proj(W, Wr)
                gc = slice(g * 128, (g + 1) * 128)
                tt("dve", s_a, zs[:, 128:256], zs[:, 256:384], ALU.mult, [zs_r], [sa_r])
                dma("sp", O["conv_s"][:, 1, gc], s_a, sa_g, reads=[sa_r])
                tt("dve", s_b, s_a, cw4[:, 2, gc], ALU.mult, [sa_r, pa_r], [sa_r])
                tt("dve", s_b, s_b, cb4[:, gc], ALU.add, [sa_r, pa_r], [sa_r])
                tt("dve", s_c, stc[:, 1, gc], cw4[:, 1, gc], ALU.mult, [pa_r], [sa_r])
                tt("dve", s_b, s_b, s_c, ALU.add, [sa_r], [sa_r])
                tt("dve", s_c, stc[:, 0, gc], cw4[:, 0, gc], ALU.mult, [pa_r], [sa_r])
                tt("dve", s_b, s_b, s_c, ALU.add, [sa_r], [sa_r])
                tt("dve", s_b, s_b, zs[:, 0:128], ALU.mult, [sa_r, zs_r], [sa_r])
                tt("dve", s_yb, s_b, sgs, ALU.mult, [sa_r, zs_r], [sa_r])
                yield
                yield
                sample_to_ysT(s_yb, sa_r, g, 1)
            start_tail(tailA(g, W, Wr))
        dma("sp", O["conv_s"][:, 0, :], I["st_conv"][:, 1, :], g_out)
    flush_pending()
    S.barrier()
    A.release(mA)

    qT = A.alloc([SEQ], BF16)
    kT = A.alloc([SEQ], BF16)
    gbs = A.alloc([SEQ], BF16)
    qkv_r = [[Res("q%d" % t), Res("k%d" % t), Res("v%d" % t), Res("gb%d" % t)] for t in range(NT)]
    Vblk = A.alloc([3, 32, 128], BF16)
    vb_r = [[Res("vb%d_%d" % (p, j)) for j in range(4)] for p in range(3)]
    kvo = [A.alloc([256], F32) for _ in range(2)]
    kvo_r = [Res("kvo0"), Res("kvo1")]
    kvo_g = [S.grp("kvo", out=True) for _ in range(2)]
    m_acc = A.mark()
    acc = A.alloc([2, WB], F32)
    acc_r = Res("acc")
    A.release(m_acc)
    vT = A.alloc([SEQ], BF16)
    A.release(m_acc)
    A.alloc([2, WB], F32)
    Kc = A.alloc([NS, 3, 128], F32)
    Vc = A.alloc([NS, 3, 2, 65], F32)
    kv_own = Res("kvown")
    memset("pool", Vc[:, :, :, :, 64:65], 1.0, [kv_own])
    Etab = A.alloc([2, 3, 2, 512], BF16)
    Etab_r = Res("Etab")
    pTt = [A.alloc([512], BF16) for _ in range(4)]
    pTt_r = [Res("pT%d" % i) for i in range(4)]
    ybt = A.alloc([WB], BF16)
    yb_r = Res("ybt")
    yb_g = S.grp("yb", out=True)
    kc_g = S.grp("kc")
    prod = A.alloc([NS, 128], F32)
    sc = A.alloc([NS, 3, 2], F32)
    pz = A.alloc([NS, 6, 4], F32)
    dr = Res("dec")
    sf = A.alloc([8], F32, parts=4)
    ot = A.alloc([128], F32, parts=4)
    slopes = [2.0 ** (-8.0 * (h + 1) / 16.0) for h in range(16)]
    ck_v = [I["ck"].rearrange("s (j d) c -> j d s c", d=d) for (_, d) in PATTERNS]
    cv_v = [I["cv"].rearrange("s (j d) c -> j d s c", d=d) for (_, d) in PATTERNS]
    sctr = 0
    if stage >= 3:
        pfB = Prefetch(P0, w_in_v, [[(4 + s) * D + g * 128 for s in range(4)] for g in range(8)], 128)
        for g in range(8):
            gc = slice(g * 128, (g + 1) * 128)
            W, Wr = pfB.get()
            for hh in range(2):
                for pi, (w_, d) in enumerate(PATTERNS):
                    c_ = float(slopes[2 * g + hh] * d)
                    if d == 1:
                        srcs = (cst["nd"][:, 0, :], cst["nd"][:, 1, :])
                    else:
                        srcs = (cst["nd_ff"], cst["nd"][:, 1, :])
                    for v_, src_ in enumerate(srcs):
                        act(Etab[:, hh, pi, v_, :], src_, AF.Exp, [cres], [Etab_r], scale=c_)
            for tt_ in range(NT):
                tk = slice(tt_ * 512, (tt_ + 1) * 512)
                for s in range(4):
                    p_, r_ = bank()
                    for k in range(8):
                        mm(p_[:, :], W[:, k, s * 128:(s + 1) * 128], hT[:, k, tk], k == 0, k == 7,
                           [Wr[0]] + hT_reads(tt_ * 4, tt_ * 4 + 4), [r_])
                    if s == 0:
                        act(qT[:, tk], p_[:, :], AF.Copy, [r_], [qkv_r[tt_][0]], scale=0.125)
                    elif s == 1:
                        cp("dve", kT[:, tk], p_[:, :], [r_], [qkv_r[tt_][1]])
                    elif s == 2:
                        cp("dve", vT[:, tk], p_[:, :], [r_], [acc_r])
                    else:
                        act(gbs[:, tk], p_[:, :], AF.Silu, [r_], [qkv_r[tt_][3]])
                step_pending()
            kvall_g = Res("kvall")
            for pi, (w_, d) in enumerate(PATTERNS):
                j0 = WB // d - 128
                dma("sp", Kc[:, :, pi, :], ck_v[pi][j0:j0 + 128, 0, :, gc], kc_g, writes=[kv_own] if pi == 0 else [Res()])
                for hh in range(2):
                    dma("sp", Vc[:, :, pi, hh, 0:64],
                        cv_v[pi][j0:j0 + 128, 0, :, g * 128 + hh * 64:g * 128 + (hh + 1) * 64], kc_g, writes=[Res()])
            kvall_g.w = kc_g.last
            kv_own.w = kc_g.last
            for t16 in range(16):
                ti = 16 + t16
                i2 = t16 % 2
                p_, r_ = bank()
                for k in range(8):
                    mm(p_[:, 0:256], hT[:, k, ti * 128:(ti + 1) * 128], W[:, k, 128:384], k == 0, k == 7,
                       [Wr[0]] + hT_reads(ti, ti + 1), [r_])
                cp("act", kvo[i2], p_[:, 0:256], [r_], [kvo_r[i2]])
                dma("sp", O["k_p"][t16 * 128:(t16 + 1) * 128, gc], kvo[i2][:, 0:128], kvo_g[i2], reads=[kvo_r[i2]])
                dma("sp", O["v_p"][t16 * 128:(t16 + 1) * 128, gc], kvo[i2][:, 128:256], kvo_g[i2], reads=[kvo_r[i2]])
            for pi, (w_, d) in enumerate(PATTERNS):
                vv = vT.rearrange("p (n i r) -> p n r i", i=128, r=d)
                for b8 in range(4):
                    pT, rT = bank()
                    pTb = pT[:].bitcast(BF16)
                    for j in range(8):
                        blk = b8 * 8 + j
                        n, r = blk // d, blk % d
                        tr(pTb[:, j * 128:(j + 1) * 128], vv[:, n, r, :], ident_b, [acc_r, cres], [rT])
                    cp("act" if b8 % 2 == 0 else "dve", Vblk[:, pi, b8 * 8:(b8 + 1) * 8, :], view(pTb[:, 0:1024], [8, 128]),
                       [rT], [vb_r[pi][b8]])
            allq = [qkv_r[t][0] for t in range(NT)]
            allk = [qkv_r[t][1] for t in range(NT)]
            deferred = []
            for sp in range(2):
                jobs = []
                for pi, (w_, d) in enumerate(PATTERNS):
                    nper = 2048 // (128 * d)
                    if d == 1:
                        for n in range(0, 16, 2):
                            jobs.append((pi, d, (sp * 16 + n, 0), (sp * 16 + n + 1, 0)))
                    else:
                        for nl in range(nper):
                            for r in range(0, d, 2):
                                jobs.append((pi, d, (sp * nper + nl, r), (sp * nper + nl, r + 1)))
                units = [(j, hh) for j in range(len(jobs)) for hh in range(2)]
                qvs = [qT.rearrange("p (n i r) -> p n r i", i=128, r=d) for (_, d) in PATTERNS]
                kvs = [kT.rearrange("p (n i r) -> p n r i", i=128, r=d) for (_, d) in PATTERNS]
                accvs = [acc.rearrange("p o (n i r) -> p o n r i", i=128, r=d) for (_, d) in PATTERNS]

                def s_stage(j):
                    nonlocal sctr
                    pi, d, b0, b1 = jobs[j]
                    bk = [bank(), bank()]
                    for qi, (n, r) in enumerate((b0, b1)):
                        for role in range(2):
                            kn = n if (role == 1 or n == 0) else n - 1
                            col = (qi * 2 + role) * 128
                            for hh in range(2):
                                hp = slice(hh * 64, (hh + 1) * 64)
                                mm(bk[hh][0][:, col:col + 128], kvs[pi][hp, kn, r, :], qvs[pi][hp, n, r, :], True, True,
                                   allq + allk, [bk[hh][1]])
                    v_ = 0 if b0[0] == 0 else 1
                    bufs = []
                    for hh in range(2):
                        i4 = sctr % 4
                        sctr += 1
                        act(pTt[i4], bk[hh][0][:, :], AF.Exp, [bk[hh][1]], [pTt_r[i4]])
                        tt("dve", pTt[i4], pTt[i4], Etab[:, hh, pi, v_, :], ALU.mult, [pTt_r[i4], Etab_r], [pTt_r[i4]])
                        bufs.append(i4)
                    return bufs

                def pv_stage(j, bufs, pOL, rOL):
                    pi, d, b0, b1 = jobs[j]
                    for qi, (n, r) in enumerate((b0, b1)):
                        for role in range(2):
                            kn = n if (role == 1 or n == 0) else n - 1
                            blk = kn * d + r
                            col = (qi * 2 + role) * 128
                            for hh in range(2):
                                hp = slice(hh * 64, (hh + 1) * 64)
                                i4 = bufs[hh]
                                mm(pOL[hp, qi * 128:(qi + 1) * 128], Vblk[:, pi, blk, hp], pTt[i4][:, col:col + 128],
                                   role == 0, role == 1, [vb_r[pi][blk // 8], pTt_r[i4]], [rOL])
                    for role in range(2):
                        for hh in range(2):
                            hp = slice(hh * 64, (hh + 1) * 64)
                            i4 = bufs[hh]
                            pv4 = view(pTt[i4], [2, 2, 128])
                            mm(pOL[hp, 256:512], ones_b, pv4[:, :, role, :], role == 0, role == 1, [cres, pTt_r[i4]], [rOL])

                def acc_stage(j, pOL, rOL):
                    pi, d, b0, b1 = jobs[j]
                    nper = 2048 // (128 * d)
                    (n0_, r0), (n1_, r1) = b0, b1
                    if d == 1:
                        nl = n0_ - sp * 16
                        dst = accvs[pi][:, :, nl:nl + 2, 0, :]
                    else:
                        nl = n0_ - sp * nper
                        dst = accvs[pi][:, :, nl, r0:r0 + 2, :]
                    src = view(pOL[:, :], [2, 2, 128])
                    if pi == 0:
                        cp("dve", dst, src, [rOL], [acc_r])
                    else:
                        tt("dve", dst, src, dst, ALU.add, [rOL, acc_r], [acc_r])

                nxt = s_stage(0)
                for j in range(len(jobs)):
                    bufs = nxt
                    if j + 1 < len(jobs):
                        nxt = s_stage(j + 1)
                    pOL, rOL = bank()
                    pv_stage(j, bufs, pOL, rOL)
                    if j == 0 and deferred:
                        deferred.pop()()
                    acc_stage(j, pOL, rOL)

                def norm(sp=sp):
                    tks = slice(sp * WB, (sp + 1) * WB)
                    act(acc[:, 1, :], acc[:, 1, :], AF.Ln, [acc_r], [acc_r])
                    act(acc[:, 1, :], acc[:, 1, :], AF.Exp, [acc_r], [acc_r], scale=-1.0)
                    tt("pool", acc[:, 0, :], acc[:, 0, :], acc[:, 1, :], ALU.mult, [acc_r], [acc_r])
                    tt("pool", ybt, acc[:, 0, :], gbs[:, tks], ALU.mult, [acc_r] + [qkv_r[t][3] for t in range(NT)], [yb_r])
                    dma("sp", Ysc[8 * sp:8 * sp + 8, :, 8 + g, :].rearrange("t p n -> p t n"), view(ybt, [8, 256]), yb_g,
                        reads=[yb_r])
                if sp == 0:
                    deferred.append(norm)
                else:
                    norm()
            def tailB(g, W, Wr, kvall):
                gc = slice(g * 128, (g + 1) * 128)
                sample_proj(W, Wr)
                dma("sp", O["k_s"][:, gc], zs[:, 128:256], zs_g, reads=[zs_r])
                dma("sp", O["v_s"][:, gc], zs[:, 256:384], zs_g, reads=[zs_r])
                yield
                pQ, rQ = bank()
                for s in range(NS):
                    mm(pQ[:, s * 128:(s + 1) * 128], cst["sel4"][:, s, :], zs[:, 0:128], True, True, [cres, zs_r], [rQ])
                for pi in range(3):
                    tt("dve", prod, Kc[:, :, pi, :], view(pQ[:, :], [NS, 128]), ALU.mult, [kv_own, kvall, rQ], [dr])
                    red("dve", sc[:, :, pi, :], prod.rearrange("p s (h e) -> p s h e", e=64), [dr], [dr])
                    if pi == 1:
                        yield
                for pi in range(3):
                    stt("dve", sc[:, :, pi, :], sc[:, :, pi, :], 0.125,
                        cst["dbias"][:, pi, 2 * g:2 * g + 2].unsqueeze(1).to_broadcast([128, NS, 2]),
                        ALU.mult, ALU.add, [dr, cres], [dr])
                act(sc.rearrange("p s q h -> p (s q h)"), sc.rearrange("p s q h -> p (s q h)"), AF.Exp, [dr], [dr])
                for s in range(NS):
                    tt("dve", pz[:, s, :, :], sc[:, s, :, :].rearrange("p q h -> p (q h)").unsqueeze(2).to_broadcast([128, 6, 4]),
                       cst["eye4"][:, s, :].unsqueeze(1).to_broadcast([128, 6, 4]), ALU.mult, [dr, cres], [dr])
                tt("dve", s_a, zs[:, 0:128], zs[:, 128:256], ALU.mult, [zs_r], [sa_r])
                red("dve", sf[:, 0:2], s_a.rearrange("p (h e) -> p h e", e=64), [sa_r], [sa_r])
                act(sf[:, 0:2], sf[:, 0:2], AF.Exp, [sa_r], [sa_r], scale=0.125)
                ts("dve", sf[:, 0:2], sf[:, 0:2], 3.0, None, ALU.mult, None, [sa_r], [sa_r])
                yield
                pO, rO = bank()
                for hh in range(2):
                    first = True
                    for s in range(NS):
                        for pi in range(3):
                            last = (s == NS - 1 and pi == 2)
                            mm(pO[0:NS, hh * 65:(hh + 1) * 65], pz[:, s, pi * 2 + hh, :], Vc[:, s, pi, hh, :], first, last,
                               [dr, kv_own, kvall], [rO])
                            first = False
                yield
                for hh in range(2):
                    he = slice(hh * 64, (hh + 1) * 64)
                    stt("dve", ot[:, he], zs[:, 256 + hh * 64:256 + (hh + 1) * 64], sf[:, hh:hh + 1], pO[0:NS, hh * 65:hh * 65 + 64],
                        ALU.mult, ALU.add, [zs_r, sa_r, rO], [sa_r])
                    tt("dve", sf[:, 2 + hh:3 + hh], sf[:, hh:hh + 1], pO[0:NS, hh * 65 + 64:hh * 65 + 65], ALU.add, [sa_r, rO], [sa_r])
                recip(sf[:, 4:6], sf[:, 2:4], [sa_r], [sa_r])
                for hh in range(2):
                    he = slice(hh * 64, (hh + 1) * 64)
                    stt("dve", s_yb[:, he], ot[:, he], sf[:, 4 + hh:5 + hh], sgs[:, he], ALU.mult, ALU.mult, [sa_r, zs_r], [sa_r])
                yield
                sample_to_ysT(s_yb, sa_r, 8 + g, 1)
            start_tail(tailB(g, W, Wr, kvall_g))
    flush_pending()
    S.barrier()
    A.release(m2)

    def out_phase(L, w_out, x_src, final):
        m = A.mark()
        n = alloc_norm(1 if not final else 2)
        og = S.grp("og")
        og_r = Res("og")
        gate_rep = load_rep(GATE[L], D, 128, og, Res())
        gsm = A.alloc([3, D], F32, parts=4)
        dma("sp", gsm, MODS[L].rearrange("a s n -> s a n"), og, writes=[Res()])
        xin = A.alloc([D], F32, parts=4)
        dma("sp", xin, I["xs"] if L == 0 else X1S, og, writes=[Res()])
        if final:
            gf_rep = load_rep(I["norm_f"], D, 128, og, Res())
            gf4 = load_rep(I["norm_f"], D, 4, og, Res())
        og_r.w = og.last
        wo = A.alloc([16, D], BF16)
        wo_rk = [Res("wo%d" % q_) for q_ in range(4)]
        wov = w_out.rearrange("(k p) n -> p k n", p=128)
        for q_ in range(4):
            wo_g = S.grp("wo")
            for k in range(4 * q_, 4 * q_ + 4):
                dma("pool", wo[:, k, :], wov[:, k, :], wo_g, writes=[Res()])
            wo_rk[q_].w = wo_g.last
        yt = [A.alloc([16, 256], BF16) for _ in range(2)]
        yt_r = [Res("yt0"), Res("yt1")]
        yt_g = [S.grp("yt") for _ in range(2)]
        x1t = [A.alloc([D], F32) for _ in range(3)]
        x1t_r = [Res("x1t%d" % i) for i in range(3)]
        x1t_g = [S.grp("x1t", out=True) for _ in range(3)]
        if final:
            ob = [A.alloc([D], F32) for _ in range(2)]
            ob_r = [Res("ob0"), Res("ob1")]
            ob_g = [S.grp("ob", out=True) for _ in range(2)]
        pend = []

        def ld_y(t2_):
            dma("sp", yt[t2_ % 2], Ysc[t2_], yt_g[t2_ % 2], writes=[yt_r[t2_ % 2]])

        def ld_x(ti_):
            dma("sp", n.xbuf[ti_ % 3], x_src[ti_ * 128:(ti_ + 1) * 128, :], n.xbuf_g[ti_ % 3], writes=[n.xbuf_r[ti_ % 3]])

        ld_y(0)
        ld_x(0)
        ld_x(1)
        for t2 in range(SEQ // 256):
            b2 = t2 % 2
            if t2 + 1 < SEQ // 256:
                ld_y(t2 + 1)
            for sub in range(2):
                ti = t2 * 2 + sub
                b3 = ti % 3
                if ti + 2 < SEQ // 128:
                    ld_x(ti + 2)
                for half in range(2):
                    pX, rX = bank()
                    for k in range(16):
                        mm(pX[:, :], yt[b2][:, k, sub * 128:(sub + 1) * 128], wo[:, k, half * 512:(half + 1) * 512], k == 0, k == 15,
                           [yt_r[b2], wo_rk[k // 4]], [rX])
                    hs = slice(half * 512, (half + 1) * 512)
                    tt("dve", x1t[b3][:, hs], pX[:, :], gate_rep[:, hs], ALU.mult, [rX, og_r], [x1t_r[b3]])
                    tt("dve", x1t[b3][:, hs], x1t[b3][:, hs], n.xbuf[b3][:, hs], ALU.add, [x1t_r[b3], n.xbuf_r[b3]], [x1t_r[b3]])
                if not final:
                    dma("sp", X1[ti * 128:(ti + 1) * 128, :], x1t[b3], x1t_g[b3], reads=[x1t_r[b3]])
                    if debug:
                        dma("sp", DBG["x1"][ti * 128:(ti + 1) * 128, :], x1t[b3], x1t_g[b3], reads=[x1t_r[b3]])
                    pend.append((ti, norm_pre(n, ti, x1t[b3], x1t_r[b3])))
                    if len(pend) > 1:
                        norm_post(n, *pend.pop(0))
                else:
                    i3 = n.ctr % 3
                    n.ctr += 1
                    o2 = ti % 2
                    memset("pool", n.ssq[i3], 0.0, [n.ssq_r[i3]])
                    S.add("act", lambda e, o_=n.sqj, i_=x1t[b3], a_=n.ssq[i3]: e.activation(out=o_, in_=i_, func=AF.Square, accum_out=a_[:, 0:1]),
                          [x1t_r[b3], n.ssq_r[i3]], [n.sqj_r, n.ssq_r[i3]])
                    rstd_from(n.ssq[i3], D, n.ssq_r[i3])
                    stt("dve", ob[o2], x1t[b3], n.ssq[i3][:, 0:1], gf_rep, ALU.mult, ALU.mult,
                        [x1t_r[b3], n.ssq_r[i3], og_r], [ob_r[o2]])
                    dma("sp", O["y_p"][ti * 128:(ti + 1) * 128, :], ob[o2], ob_g[o2], reads=[ob_r[o2]])
        while pend:
            norm_post(n, *pend.pop(0))
        s_x = A.alloc([D], F32, parts=4)
        sx_r = Res("sx")
        for half in range(2):
            pX, rX = bank()
            hs = slice(half * 512, (half + 1) * 512)
            for k in range(16):
                mm(pX[0:NS, :], ysT[:, k, :], wo[:, k, hs], k == 0, k == 15, [ysT_res, wo_rk[k // 4]], [rX])
            tt("dve", s_x[:, hs], pX[0:NS, :], gsm[:, 2, hs], ALU.mult, [rX, og_r], [sx_r])
        tt("dve", s_x, s_x, xin, ALU.add, [sx_r, og_r], [sx_r])
        if not final:
            dma("sp", X1S, s_x, S.grp("x1s", out=True), reads=[sx_r])
            if debug:
                dma("sp", DBG["x1s"], s_x, g_out, reads=[sx_r])
            norm_sample(n, s_x, sx_r)
        else:
            tt("dve", n.s_t1, s_x, s_x, ALU.mult, [sx_r], [n.sn_r])
            red("dve", n.s_ss, n.s_t1, [n.sn_r], [n.sn_r])
            rstd_from(n.s_ss, D, n.sn_r)
            stt("dve", n.s_t1, s_x, n.s_ss[:, 0:1], gf4, ALU.mult, ALU.mult, [sx_r, n.sn_r, og_r], [n.sn_r])
            dma("sp", O["y_s"], n.s_t1, g_out, reads=[n.sn_r])
        S.barrier()
        A.release(m)

    if stage >= 4:
        out_phase(0, I["w_out_e"], I["xp"], False)

    if stage >= 5:
        m4 = A.mark()
        w1v = I["w_in_o"].rearrange("(k p) n -> p k n", p=128)
        gdec = [1.0 - 2.0 ** (-5.0 - h) for h in range(4)]
        pw_bf = A.alloc([4, 2, 256], BF16)
        pw_r = Res("pw")
        pw_g = S.grp("pw")
        for gi in range(4):
            dma("pool", pw_bf[:, gi, :, :], I["pool_w"][gi].rearrange("(c p) e -> p c e", p=128), pw_g, writes=[Res()])
        pw_r.w = pw_g.last
        ps4_r = Res("ps4")
        ps4 = load_rep(I["pool_scale"], D, 4, S.grp("ps4"), ps4_r)
        zs1 = A.alloc([1024], F32, parts=4)
        sg2 = A.alloc([256], F32, parts=4)
        s_w1 = A.alloc([256], F32, parts=4)
        s_w2 = A.alloc([256], F32, parts=4)
        s_y2 = A.alloc([256], BF16, parts=4)
        mC = A.mark()
        P1 = alloc_panels(512, 4)
        ue = [[A.alloc([528], F32) for _ in range(2)] for _ in range(2)]
        ue_r = [[Res("ue%d%d" % (h, p)) for p in range(2)] for h in range(2)]
        sA = [A.alloc([528], F32) for _ in range(2)]
        sB = [A.alloc([528], F32) for _ in range(2)]
        sw_r = [Res("sw0"), Res("sw1")]
        pld = [[A.alloc([512], BF16) for _ in range(2)] for _ in range(2)]
        pld_r = [[Res("pl%d%d" % (h, p)) for p in range(2)] for h in range(2)]
        sgc = [[A.alloc([512], F32) for _ in range(2)] for _ in range(2)]
        sgc_r = [[Res("sg%d%d" % (h, p)) for p in range(2)] for h in range(2)]
        ycb = [A.alloc([512], BF16) for _ in range(2)]
        yc_r = [Res("yc0"), Res("yc1")]
        yc_g = [S.grp("yc", out=True) for _ in range(2)]
        t16 = A.alloc([16], F32)
        t16_r = Res("t16")
        ppo = A.alloc([128], F32, parts=16)
        ppo_r = Res("ppo")
        ppo_g = S.grp("ppo", out=True)
        spt = A.alloc([15, 256], F32, parts=4)
        spt_r = Res("spt")
        spt_g = S.grp("spt")
        pls = A.alloc([2, NS], BF16)
        pls_r = Res("pls")
        ps_g = S.grp("pools", out=True)
        dma("sp", O["pool_s"][:, 0:14, :], I["st_pool"][:, 1:15, :], g_out)
        pfC = Prefetch(P1, w1v, [[gi * 256, gi * 256 + 128, D + gi * 256, D + gi * 256 + 128] for gi in range(4)], 128)
        pctr = 0
        yctr = 0
        pmix = None
        for gi in range(4):
            w_ = POOL_SIZES[gi]
            W, Wr = pfC.get()
            for tt_ in range(NT):
                i2 = pctr % 2
                pctr += 1
                tk = slice(tt_ * 512, (tt_ + 1) * 512)
                pb = []
                for s in range(4):
                    p_, r_ = bank()
                    for k in range(8):
                        mm(p_[:, :], W[:, k, s * 128:(s + 1) * 128], hT[:, k, tk], k == 0, k == 7,
                           [Wr[0]] + hT_reads(tt_ * 4, tt_ * 4 + 4), [r_])
                    pb.append((p_, r_))
                for hf in range(2):
                    u = ue[hf][i2]
                    ur = ue_r[hf][i2]
                    p_u, r_u = pb[hf]
                    p_g, r_g = pb[2 + hf]
                    cp("act", u[:, 16:528], p_u[:, :], [r_u], [ur])
                    act(sgc[hf][i2], p_g[:, :], AF.Silu, [r_g], [sgc_r[hf][i2]])
                    if tt_ == 0:
                        memset("pool", u[:, 0:16], 0.0, [ur])
                    if tt_ < NT - 1:
                        cp("pool", ue[hf][1 - i2][:, 0:16], u[:, 512:528], [ur], [ue_r[hf][1 - i2]])
                    a_, b_ = sA[hf], sB[hf]
                    we = "pool" if hf == 0 else "dve"
                    tt(we, a_[:, 1:528], u[:, 1:528], u[:, 0:527], ALU.add, [ur], [sw_r[hf]])
                    fin = a_
                    if gi >= 1:
                        tt(we, b_[:, 3:528], a_[:, 3:528], a_[:, 1:526], ALU.add, [sw_r[hf]], [sw_r[hf]])
                        fin = b_
                    if gi >= 2:
                        tt(we, a_[:, 7:528], b_[:, 7:528], b_[:, 3:524], ALU.add, [sw_r[hf]], [sw_r[hf]])
                        fin = a_
                    if gi >= 3:
                        tt(we, b_[:, 15:528], a_[:, 15:528], a_[:, 7:520], ALU.add, [sw_r[hf]], [sw_r[hf]])
                        fin = b_
                    stt("dve", pld[hf][i2], fin[:, 16:528], 1.0 / w_, u[:, 16:528], ALU.mult, ALU.subtract,
                        [sw_r[hf], ur], [pld_r[hf][i2]])
                    if tt_ == 0:
                        tt("dve", t16, fin[:, 16:32], cst["rcp"][:, gi, :], ALU.mult, [sw_r[hf], cres], [t16_r])
                        tt("dve", pld[hf][i2][:, 0:16], t16, u[:, 16:32], ALU.subtract, [t16_r, ur], [pld_r[hf][i2]])
                    if tt_ == NT - 1:
                        pT_, rT_ = bank()
                        tr(pT_[0:16, 0:128], u[:, 512:528], cst["ident_f"], [ur, cres], [rT_])
                        cp("dve", ppo, pT_[0:16, 0:128], [rT_], [ppo_r])
                        ch0 = gi * 256 + hf * 128
                        dma("sp", O["pool_p"][0:15, ch0:ch0 + 128], ppo[1:16, :], ppo_g, reads=[ppo_r])
                if pmix is not None:
                    pmix()
                step_pending()

                def mk(gi=gi, tt_=tt_, i2=i2):
                    def mix():
                        nonlocal yctr
                        for eh in range(2):
                            pM, rM = bank()
                            for c in range(2):
                                mm(pM[:, :], pw_bf[:, gi, c, eh * 128:(eh + 1) * 128], pld[c][i2], c == 0, c == 1, [pw_r, pld_r[c][i2]], [rM])
                            y2 = yctr % 2
                            yctr += 1
                            ch = 2 * gi + eh
                            stt("dve", ycb[y2], pM[:, :], psT[:, ch:ch + 1], sgc[eh][i2], ALU.mult, ALU.mult, [rM, pres, sgc_r[eh][i2]], [yc_r[y2]])
                            dma("sp", Ysc[2 * tt_:2 * tt_ + 2, :, ch, :].rearrange("t p n -> p t n"), view(ycb[y2], [2, 256]), yc_g[y2], reads=[yc_r[y2]])
                    return mix
                pmix = mk()
            def tailC(gi, w_, W, Wr):
                gcs = slice(gi * 256, (gi + 1) * 256)
                sample_proj(W, Wr, 512, (256, 512), zs1, sg2)
                dma("sp", O["pool_s"][:, 14, gcs], zs1[:, 0:256], ps_g, reads=[zs_r])
                dma("sp", spt, I["st_pool"][:, :, gcs], spt_g, writes=[spt_r])
                red("dve", s_w1, spt[:, 15 - (w_ - 1):15, :].rearrange("p r c -> p c r"), [spt_r], [sa_r])
                tt("dve", s_w1, s_w1, zs1[:, 0:256], ALU.add, [sa_r, zs_r], [sa_r])
                stt("dve", s_y2, s_w1, 1.0 / w_, zs1[:, 0:256], ALU.mult, ALU.subtract, [sa_r, zs_r], [sa_r])
                yield
                pT_, rT_ = bank()
                pTb_ = pT_[:].bitcast(BF16)
                for c in range(2):
                    tr(pTb_[:, c * NS:(c + 1) * NS], s_y2[:, c * 128:(c + 1) * 128], ident_b[0:NS, 0:NS], [sa_r, cres], [rT_])
                cp("act", pls, view(pTb_[:, 0:2 * NS], [2, NS]), [rT_], [pls_r])
                yield
                pMs, rMs = bank()
                for c in range(2):
                    mm(pMs[0:NS, 0:256], pls[:, c, :], pw_bf[:, gi, c, :], c == 0, c == 1, [pls_r, pw_r], [rMs])
                tt("dve", s_w2, pMs[0:NS, 0:256], ps4[:, gcs], ALU.mult, [rMs, ps4_r], [sa_r])
                tt("dve", s_y2, s_w2, sg2, ALU.mult, [sa_r, zs_r], [sa_r])
                yield
                sample_to_ysT(s_y2, sa_r, 2 * gi, 2)
            pmix()
            pmix = None
            start_tail(tailC(gi, w_, W, Wr))
        flush_pending()
        S.barrier()
        A.release(mC)

        PD = alloc_panels(1024, 4)
        Sm = A.alloc([2, 256], F32)
        Sb = A.alloc([2, 256], BF16)
        Sm_r = Res("Sm")
        Sb_r = Res("Sb")
        qTt = [A.alloc([2, 512], BF16) for _ in range(2)]
        qsT = [A.alloc([2, 512], BF16) for _ in range(2)]
        kTt = [A.alloc([2, 512], BF16) for _ in range(2)]
        gds = [A.alloc([2, 512], BF16) for _ in range(2)]
        q_r = [[Res("qt%d%d" % (p, c)) for c in range(2)] for p in range(2)]
        qs_r = [[Res("qs%d%d" % (p, c)) for c in range(2)] for p in range(2)]
        k_r = [[Res("kt%d%d" % (p, c)) for c in range(2)] for p in range(2)]
        gd_r = [[Res("gd%d%d" % (p, c)) for c in range(2)] for p in range(2)]
        ktm = [A.alloc([4, 256], BF16) for _ in range(2)]
        vtm = [A.alloc([4, 256], BF16) for _ in range(2)]
        kv_r = [[Res("kv%d%d" % (p, c)) for c in range(4)] for p in range(2)]
        ATb = [A.alloc([128], BF16) for _ in range(2)]
        AT_r = [Res("AT0"), Res("AT1")]
        sq = A.alloc([2, 512], F32)
        sq_r = Res("sq")
        rsd = A.alloc([512], F32)
        rsd_r = Res("rsd")
        ytmp = A.alloc([512], F32)
        ytmp_r = Res("ytmp")
        ydb = [A.alloc([512], BF16) for _ in range(2)]
        yd_r = [Res("yd0"), Res("yd1")]
        yd_g = [S.grp("yd", out=True) for _ in range(2)]
        rp_g = S.grp("retp", out=True)
        Sd = A.alloc([NS, 2, 256], F32)
        Sd_r = Res("Sd")
        Sd_g = S.grp("Sd")
        rs_g = S.grp("rets", out=True)
        ksel = A.alloc([NS, 256], F32, parts=4)
        ks_r = Res("ksel")
        qsel = A.alloc([2, NS, 4], F32)
        qsel_r = Res("qsel")
        o4s = A.alloc([256], F32, parts=4)
        pfD = Prefetch(PD, w1v, [[2 * D + hd * 256, 3 * D + hd * 256, 4 * D + hd * 256, 5 * D + hd * 256] for hd in range(4)], 256)
        rot[0] = [0, 1, 2, 3, 4, 5]
        pOt = [banks[6], banks[7]]
        pOt_r = [bank_res[6], bank_res[7]]
        tctr = 0
        actr = 0
        ydc = 0
        pfin = None
        for hd in range(4):
            W, Wr = pfD.get()
            g128 = float(gdec[hd] ** 128)
            memset("dve", Sm, 0.0, [Sm_r])
            memset("dve", Sb, 0.0, [Sb_r])
            for tt_ in range(NT):
                i2 = tctr % 2
                tctr += 1
                tk = slice(tt_ * 512, (tt_ + 1) * 512)
                hr = hT_reads(tt_ * 4, tt_ * 4 + 4)
                for sec, kind in ((0, "q"), (1, "k"), (3, "g")):
                    for c in range(2):
                        p_, r_ = bank()
                        c0 = sec * 256 + c * 128
                        for k in range(8):
                            mm(p_[:, :], W[:, k, c0:c0 + 128], hT[:, k, tk], k == 0, k == 7, [Wr[0]] + hr, [r_])
                        if kind == "q":
                            cp("act", qTt[i2][:, c, :], p_[:, :], [r_], [q_r[i2][c]])
                            tt("pool", view(qsT[i2][:, c, :], [4, 128]), view(qTt[i2][:, c, :], [4, 128]),
                               cst["qsc"][:, hd, :].unsqueeze(1).to_broadcast([128, 4, 128]), ALU.mult, [q_r[i2][c], cres], [qs_r[i2][c]])
                        elif kind == "k":
                            act(kTt[i2][:, c, :], p_[:, :], AF.Copy, [r_], [k_r[i2][c]], scale=1.0 / 16.0)
                        else:
                            act(gds[i2][:, c, :], p_[:, :], AF.Silu, [r_], [gd_r[i2][c]])
                step_pending()
                if pfin is not None:
                    pfin()
                    pfin = None
                for c4 in range(4):
                    ti = tt_ * 4 + c4
                    p_, r_ = bank()
                    for k in range(8):
                        mm(p_[:, :], hT[:, k, ti * 128:(ti + 1) * 128], W[:, k, 256:768], k == 0, k == 7, [Wr[0]] + hT_reads(ti, ti + 1), [r_])
                    ts("dve", ktm[i2][:, c4, :], p_[:, 0:256], cst["kdec"][:, hd:hd + 1], None, ALU.mult, None, [r_, cres], [kv_r[i2][c4]])
                    cp("dve", vtm[i2][:, c4, :], p_[:, 256:512], [r_], [kv_r[i2][c4]])
                def front(c4, i2=i2):
                    nonlocal actr
                    cs = slice(c4 * 128, (c4 + 1) * 128)
                    a2 = actr % 2
                    actr += 1
                    pSc, rSc = bank()
                    for dkc in range(2):
                        mm(pSc[:, 0:128], kTt[i2][:, dkc, cs], qTt[i2][:, dkc, cs], dkc == 0, dkc == 1, [k_r[i2][dkc], q_r[i2][dkc]], [rSc])
                    pSt, rSt = bank()
                    for dkc in range(2):
                        mm(pSt[:, dkc * 256:(dkc + 1) * 256], ktm[i2][:, c4, dkc * 128:(dkc + 1) * 128], vtm[i2][:, c4, :], True, True,
                           [kv_r[i2][c4]], [rSt])
                    tt("dve", ATb[a2], pSc[:, 0:128], cst["decT"][:, hd, :], ALU.mult, [rSc, cres], [AT_r[a2]])
                    return a2, pSt, rSt

                fr = front(0)
                for c4 in range(4):
                    cs = slice(c4 * 128, (c4 + 1) * 128)
                    a2, pSt, rSt = fr
                    if c4 + 1 < 4:
                        fr = front(c4 + 1)
                    for dvc in range(2):
                        dv = slice(dvc * 128, (dvc + 1) * 128)
                        mm(pOt[dvc][:, cs], vtm[i2][:, c4, dv], ATb[a2], True, False, [kv_r[i2][c4], AT_r[a2]], [pOt_r[dvc]])
                        mm(pOt[dvc][:, cs], Sb[:, 0, dv], qsT[i2][:, 0, cs], False, False, [Sb_r, qs_r[i2][0]], [pOt_r[dvc]])
                        mm(pOt[dvc][:, cs], Sb[:, 1, dv], qsT[i2][:, 1, cs], False, True, [Sb_r, qs_r[i2][1]], [pOt_r[dvc]])
                    stt("dve", Sm.rearrange("p c v -> p (c v)"), Sm.rearrange("p c v -> p (c v)"), g128, pSt[:, :],
                        ALU.mult, ALU.add, [Sm_r, rSt], [Sm_r])
                    cp("act", Sb, Sm, [Sm_r], [Sb_r])
                for dvc in range(2):
                    act(sq[:, dvc, :], pOt[dvc][:, :], AF.Square, [pOt_r[dvc]], [sq_r])

                def mk_fin(hd=hd, tt_=tt_, i2=i2):
                    def fin():
                        nonlocal ydc
                        pSS, rSS = bank()
                        for dvc in range(2):
                            mm(pSS[:, :], ones_f, sq[:, dvc, :], dvc == 0, dvc == 1, [cres, sq_r], [rSS])
                        act(rsd, pSS[:, :], AF.Ln, [rSS, cres], [rsd_r], bias=eps_t[:, 0:1], scale=1.0 / 256.0)
                        act(rsd, rsd, AF.Exp, [rsd_r], [rsd_r], scale=-0.5)
                        for dvc in range(2):
                            y2 = ydc % 2
                            ydc += 1
                            tt("dve", ytmp, pOt[dvc][:, :], rsd, ALU.mult, [pOt_r[dvc], rsd_r], [ytmp_r])
                            tt("dve", ydb[y2], ytmp, gds[i2][:, dvc, :], ALU.mult, [ytmp_r, gd_r[i2][dvc]], [yd_r[y2]])
                            dma("sp", Ysc[2 * tt_:2 * tt_ + 2, :, 8 + 2 * hd + dvc, :].rearrange("t p n -> p t n"), view(ydb[y2], [2, 256]),
                                yd_g[y2], reads=[yd_r[y2]])
                    return fin
                pfin = mk_fin()
            pfin()
            pfin = None
            dma("sp", O["ret_p"][hd].rearrange("(c p) v -> p c v", p=128), Sm, rp_g, reads=[Sm_r])
            def tailD(hd, W, Wr):
                sample_proj(W, Wr, 1024, (768, 1024), zs1, sg2)
                for s in range(NS):
                    dma("sp", Sd[:, s, :, :], I["st_ret"][s, hd].rearrange("(c p) v -> p c v", p=128), Sd_g,
                        writes=[Sd_r] if s == 0 else [Res()])
                Sd_r.w = Sd_g.last
                Sd_r.rs = []
                for s in range(NS):
                    ts("dve", ksel[:, s, :], zs1[:, 256:512], cst["ident_f"][0:NS, s:s + 1], 1.0 / 16.0, ALU.mult, ALU.mult, [zs_r, cres], [ks_r])
                yield
                for s in range(NS):
                    for dkc in range(2):
                        pSt, rSt = bank()
                        mm(pSt[:, 0:256], ksel[0:NS, s, dkc * 128:(dkc + 1) * 128], zs1[0:NS, 512:768], True, True, [ks_r, zs_r], [rSt])
                        stt("dve", Sd[:, s, dkc, :], Sd[:, s, dkc, :], float(gdec[hd]), pSt[:, 0:256], ALU.mult, ALU.add, [Sd_r, rSt], [Sd_r])
                    if s == 1:
                        yield
                for s in range(NS):
                    dma("sp", O["ret_s"][s, hd].rearrange("(c p) v -> p c v", p=128), Sd[:, s, :, :], rs_g, reads=[Sd_r])
                pTq, rTq = bank()
                for dkc in range(2):
                    tr(pTq[:, dkc * NS:(dkc + 1) * NS], zs1[0:NS, dkc * 128:(dkc + 1) * 128], cst["ident_f"][0:NS, 0:NS], [zs_r, cres], [rTq])
                for dkc in range(2):
                    tt("dve", qsel[:, dkc, :, :], pTq[:, dkc * NS:(dkc + 1) * NS].unsqueeze(2).to_broadcast([128, NS, 4]), cst["eye4"], ALU.mult,
                       [rTq, cres], [qsel_r])
                yield
                pOs, rOs = bank()
                n_ = 0
                for s in range(NS):
                    for dkc in range(2):
                        mm(pOs[0:NS, 0:256], qsel[:, dkc, s, :], Sd[:, s, dkc, :], n_ == 0, n_ == 2 * NS - 1, [qsel_r, Sd_r], [rOs])
                        n_ += 1
                cp("act", o4s, pOs[0:NS, 0:256], [rOs], [sa_r])
                tt("dve", s_w1, o4s, o4s, ALU.mult, [sa_r], [sa_r])
                red("dve", sf[:, 0:1], s_w1, [sa_r], [sa_r])
                rstd_from(sf[:, 0:1], 256, sa_r)
                stt("dve", s_y2, o4s, sf[:, 0:1], sg2, ALU.mult, ALU.mult, [sa_r, zs_r], [sa_r])
                yield
                sample_to_ysT(s_y2, sa_r, 8 + 2 * hd, 2)
            start_tail(tailD(hd, W, Wr))
        flush_pending()
        rot[0] = list(range(8))
        S.barrier()
        A.release(m4)
    if stage >= 6:
        out_phase(1, I["w_out_o"], X1, True)

    S.emit()
    st.close()
    return nc


_CACHE = {}
PROMPT_CORES = [0, 1, 4, 5]


def _prep_inputs(inputs):
    consts = make_consts()
    maps = []
    f = lambda a: np.ascontiguousarray(np.asarray(a, dtype=np.float32))
    shared = {
        "norm_e": f(inputs["norm_e"][0]), "ada_w_e": f(inputs["ada_w_e"][0]), "ada_b_e": f(inputs["ada_b_e"][0]),
        "w_in_e": f(inputs["w_in_e"][0]), "conv_w": f(inputs["conv_w"][0]), "conv_b": f(inputs["conv_b"][0]),
        "w_out_e": f(inputs["w_out_e"][0]), "norm_o": f(inputs["norm_o"][0]), "ada_w_o": f(inputs["ada_w_o"][0]),
        "ada_b_o": f(inputs["ada_b_o"][0]), "w_in_o": f(inputs["w_in_o"][0]), "pool_w": f(inputs["pool_w"][0]),
        "pool_scale": f(inputs["pool_scale"][0]), "w_out_o": f(inputs["w_out_o"][0]), "norm_f": f(inputs["norm_f"]),
    }
    for k, v in consts.items():
        shared["c_" + k] = f(v)
    zx = np.zeros((SEQ, D), np.float32)
    zc = np.zeros((1, D), np.float32)
    for c in range(NCORES):
        ss = slice(c * NS, (c + 1) * NS)
        m = dict(shared)
        if c in PROMPT_CORES:
            b = PROMPT_CORES.index(c)
            m["xp"] = f(inputs["x_prompt"][b])
            m["cp"] = f(inputs["c_prompt"][b:b + 1])
        else:
            m["xp"] = zx
            m["cp"] = zc
        m["xs"] = f(inputs["x_sample"][ss, 0])
        m["cs"] = f(inputs["c_sample"][ss])
        m["st_conv"] = f(inputs["state_conv"][0, ss])
        m["ck"] = f(np.asarray(inputs["cache_win_k"])[0, ss].reshape(NS, WB, D))
        m["cv"] = f(np.asarray(inputs["cache_win_v"])[0, ss].reshape(NS, WB, D))
        m["st_pool"] = f(inputs["state_pool"][0, ss])
        m["st_ret"] = f(inputs["state_ret"][0, ss])
        maps.append(m)
    return maps


def _run(inputs, stage=99, debug=False):
    key = (stage, debug)
    if key not in _CACHE:
        _CACHE[key] = build_program(stage, debug)
    nc = _CACHE[key]
    maps = _prep_inputs(inputs)
    res = run_bass_kernel_spmd(nc, maps, core_ids=list(range(NCORES)))
    return res.results


def kernel(**inputs):
    r = _run(inputs)
    g = lambda name, c: np.asarray(r[c][name], dtype=np.float32)
    y_p = np.stack([g("y_p", b) for b in PROMPT_CORES])
    y_s = np.concatenate([g("y_s", c) for c in range(NCORES)])[:, None, :]
    conv_p = np.stack([g("conv_p", b) for b in PROMPT_CORES])[None]
    conv_s = np.concatenate([g("conv_s", c) for c in range(NCORES)])[None]
    k_p = np.stack([g("k_p", b).reshape(WB, 16, 64) for b in PROMPT_CORES])[None]
    k_s = np.concatenate([g("k_s", c) for c in range(NCORES)]).reshape(1, 32, 1, 16, 64)
    v_p = np.stack([g("v_p", b).reshape(WB, 16, 64) for b in PROMPT_CORES])[None]
    v_s = np.concatenate([g("v_s", c) for c in range(NCORES)]).reshape(1, 32, 1, 16, 64)
    pool_p = np.stack([g("pool_p", b) for b in PROMPT_CORES])[None]
    pool_s = np.concatenate([g("pool_s", c) for c in range(NCORES)])[None]
    ret_p = np.stack([g("ret_p", b) for b in PROMPT_CORES])[None]
    ret_s = np.concatenate([g("ret_s", c) for c in range(NCORES)])[None]
    return (y_p, y_s, conv_p, conv_s, k_p, k_s, v_p, v_s, pool_p, pool_s, ret_p, ret_s)
```
